# Optimizing a Trainium2 kernel written in Bass

```python
import jax, jax.numpy as jnp
from jax import lax
import numpy as np

D_MODEL = 1024
BATCH = 1
SEQ = 16384
DEPTH = 1
DEC_BATCH = 16
DEC_SEQ = 2048
PAST_LEN = 128

HEAD_DIM = 64
N_HEADS_A = 8
N_HEADS_B = 8
N_KV_B = 2
WIDTH_A = N_HEADS_A * HEAD_DIM
WIDTH_B = N_HEADS_B * HEAD_DIM
KV_WIDTH_B = N_KV_B * HEAD_DIM
MIX_WIDTH = WIDTH_A + WIDTH_B
IN_WIDTH = 3 * WIDTH_A + WIDTH_B + 2 * KV_WIDTH_B
DILATED_CONFIGS = ((128, 1), (512, 4), (2048, 16))
WINDOW_B = 128
D_FF = 2816
CONV_WIDTH = 3
EPS = 1e-6
NEG_INF = -1e30

kernel_name = 'hymba_dilated_swa_convffn_encoder'


def alibi_slopes():
    n = N_HEADS_A + N_HEADS_B
    h = np.arange(1, n + 1, dtype=np.float32)
    s = np.exp2(-8.0 * h / n).astype(np.float32)
    return jnp.asarray(s[N_HEADS_B:]), jnp.asarray(s[:N_HEADS_B])


def rms_norm(x, g):
    xf = x.astype(jnp.float32)
    y = xf * lax.rsqrt(jnp.mean(xf * xf, axis=-1, keepdims=True) + EPS)
    return (y * g.astype(jnp.float32)).astype(x.dtype)


def banded_attention(q, k, v, key_valid, radius, dist_scale, slopes, sink=None):
    n, L, G, R, dh = q.shape
    blk = radius
    nb = -(-L // blk)
    pad = nb * blk - L
    q = jnp.pad(q, ((0, 0), (0, pad), (0, 0), (0, 0), (0, 0)))
    kv_pad = ((0, 0), (blk, pad + blk), (0, 0), (0, 0))
    k = jnp.pad(k, kv_pad)
    v = jnp.pad(v, kv_pad)
    valid = jnp.pad(key_valid, ((0, 0), (blk, pad + blk)))

    def windows(a):
        a = a.reshape((n, nb + 2, blk) + a.shape[2:])
        return jnp.concatenate([a[:, :-2], a[:, 1:-1], a[:, 2:]], axis=2)

    kw, vw, mw = windows(k), windows(v), windows(valid)
    qb = q.reshape(n, nb, blk, G, R, dh)
    s = jnp.einsum('nbqgrd,nbkgd->nbgrqk', qb, kw).astype(jnp.float32) * (dh ** -0.5)
    rel = jnp.arange(3 * blk)[None, :] - blk - jnp.arange(blk)[:, None]
    dist = jnp.abs(rel).astype(jnp.float32) * dist_scale
    s = s - slopes.astype(jnp.float32)[:, :, None, None] * dist
    allowed = (jnp.abs(rel) <= radius)[None, None, None, None] & mw[:, :, None, None, None, :]
    s = jnp.where(allowed, s, NEG_INF)
    m = jnp.max(s, axis=-1)
    if sink is not None:
        sk = sink.astype(jnp.float32)[None, None, :, :, None]
        m = jnp.maximum(m, sk)
    p = jnp.exp(s - m[..., None])
    denom = jnp.sum(p, axis=-1)
    if sink is not None:
        denom = denom + jnp.exp(sk - m)
    o = jnp.einsum('nbgrqk,nbkgd->nbqgrd', p.astype(v.dtype), vw).astype(jnp.float32)
    denom_t = jnp.transpose(denom, (0, 1, 4, 2, 3))
    m_t = jnp.transpose(m, (0, 1, 4, 2, 3))
    o = o / denom_t[..., None]
    lse = m_t + jnp.log(denom_t)
    o = o.reshape(n, nb * blk, G, R, dh)[:, :L]
    lse = lse.reshape(n, nb * blk, G, R)[:, :L]
    return o, lse


def dilated_attention(q, k, v, slopes):
    b, S, H, dh = q.shape
    outs, lses = [], []
    for window, d in DILATED_CONFIGS:
        radius = window // (2 * d)
        Sd = -(-S // d) * d
        pad = ((0, 0), (0, Sd - S), (0, 0), (0, 0))

        def split(a):
            a = a.reshape((b, Sd // d, d) + a.shape[2:])
            a = jnp.swapaxes(a, 1, 2)
            return a.reshape((b * d, Sd // d) + a.shape[3:])

        def merge(a):
            a = a.reshape((b, d, Sd // d) + a.shape[2:])
            a = jnp.swapaxes(a, 1, 2)
            return a.reshape((b, Sd) + a.shape[3:])[:, :S]

        valid = jnp.broadcast_to(jnp.arange(Sd) < S, (b, Sd))
        qs = split(jnp.pad(q, pad))[:, :, :, None]
        ks = split(jnp.pad(k, pad))
        vs = split(jnp.pad(v, pad))
        o, lse = banded_attention(qs, ks, vs, split(valid), radius, float(d), slopes[:, None])
        outs.append(merge(o[:, :, :, 0]))
        lses.append(merge(lse[:, :, :, 0]))
    w = jax.nn.softmax(jnp.stack(lses, axis=0), axis=0)
    return jnp.sum(w[..., None] * jnp.stack(outs, axis=0), axis=0)


def encoder_layer(x, norm1, w_in, q_norm_a, k_norm_a, q_norm_b, k_norm_b, sink_b,
                  out_norm_a, out_norm_b, w_out, norm2, w_up, conv_w, conv_b, w_down):
    b, S, _ = x.shape
    slopes_a, slopes_b = alibi_slopes()
    h = rms_norm(x, norm1)
    proj = h @ w_in
    cuts = [WIDTH_A, 2 * WIDTH_A, 3 * WIDTH_A, 3 * WIDTH_A + WIDTH_B, 3 * WIDTH_A + WIDTH_B + KV_WIDTH_B]
    qa, ka, va, qb, kb, vb = jnp.split(proj, cuts, axis=-1)

    qa = rms_norm(qa.reshape(b, S, N_HEADS_A, HEAD_DIM), q_norm_a)
    ka = rms_norm(ka.reshape(b, S, N_HEADS_A, HEAD_DIM), k_norm_a)
    va = va.reshape(b, S, N_HEADS_A, HEAD_DIM)
    ya = dilated_attention(qa, ka, va, slopes_a).astype(x.dtype).reshape(b, S, WIDTH_A)

    rep = N_HEADS_B // N_KV_B
    qb = rms_norm(qb.reshape(b, S, N_KV_B, rep, HEAD_DIM), q_norm_b)
    kb = rms_norm(kb.reshape(b, S, N_KV_B, HEAD_DIM), k_norm_b)
    vb = vb.reshape(b, S, N_KV_B, HEAD_DIM)
    valid = jnp.ones((b, S), dtype=bool)
    yb, _ = banded_attention(qb, kb, vb, valid, WINDOW_B, 1.0,
                             slopes_b.reshape(N_KV_B, rep), sink_b.reshape(N_KV_B, rep))
    yb = yb.astype(x.dtype).reshape(b, S, WIDTH_B)

    y = jnp.concatenate([rms_norm(ya, out_norm_a), rms_norm(yb, out_norm_b)], axis=-1)
    x = x + y @ w_out

    u = rms_norm(x, norm2) @ w_up
    half = CONV_WIDTH // 2
    up = jnp.pad(u, ((0, 0), (half, half), (0, 0)))
    c = conv_b
    for j in range(CONV_WIDTH):
        c = c + up[:, j:j + S] * conv_w[j]
    gate, val = jnp.split(c, 2, axis=-1)
    return x + (jax.nn.silu(gate) * val) @ w_down


def setup_inputs(seed: int = 0) -> dict:
    key = jax.random.key(seed)
    ks = jax.random.split(key, 20)
    f32 = jnp.float32

    def nrm(k, shape, scale):
        return jax.random.normal(k, shape, f32) * scale

    def gain(k, shape):
        return 1.0 + 0.02 * jax.random.normal(k, shape, f32)

    return {
        'x_prompt': nrm(ks[0], (BATCH, SEQ, D_MODEL), 1.0),
        'x_sample': nrm(ks[1], (DEC_BATCH, DEC_SEQ, D_MODEL), 1.0),
        'norm1': gain(ks[2], (DEPTH, D_MODEL)),
        'w_in': nrm(ks[3], (DEPTH, D_MODEL, IN_WIDTH), D_MODEL ** -0.5),
        'q_norm_a': gain(ks[4], (DEPTH, HEAD_DIM)),
        'k_norm_a': gain(ks[5], (DEPTH, HEAD_DIM)),
        'q_norm_b': gain(ks[6], (DEPTH, HEAD_DIM)),
        'k_norm_b': gain(ks[7], (DEPTH, HEAD_DIM)),
        'sink_b': nrm(ks[8], (DEPTH, N_HEADS_B), 0.1),
        'out_norm_a': gain(ks[9], (DEPTH, WIDTH_A)),
        'out_norm_b': gain(ks[10], (DEPTH, WIDTH_B)),
        'w_out': nrm(ks[11], (DEPTH, MIX_WIDTH, D_MODEL), MIX_WIDTH ** -0.5),
        'norm2': gain(ks[12], (DEPTH, D_MODEL)),
        'w_up': nrm(ks[13], (DEPTH, D_MODEL, 2 * D_FF), D_MODEL ** -0.5),
        'conv_w': nrm(ks[14], (DEPTH, CONV_WIDTH, 2 * D_FF), CONV_WIDTH ** -0.5),
        'conv_b': nrm(ks[15], (DEPTH, 2 * D_FF), 0.02),
        'w_down': nrm(ks[16], (DEPTH, D_FF, D_MODEL), D_FF ** -0.5),
    }


def reference(x_prompt, x_sample, norm1, w_in, q_norm_a, k_norm_a, q_norm_b, k_norm_b, sink_b,
              out_norm_a, out_norm_b, w_out, norm2, w_up, conv_w, conv_b, w_down):
    y_prompt = x_prompt
    y_sample = x_sample
    for l in range(DEPTH):
        layer_args = (norm1[l], w_in[l], q_norm_a[l], k_norm_a[l], q_norm_b[l], k_norm_b[l], sink_b[l],
                      out_norm_a[l], out_norm_b[l], w_out[l], norm2[l], w_up[l], conv_w[l], conv_b[l], w_down[l])
        y_prompt = encoder_layer(y_prompt, *layer_args)
        y_sample = encoder_layer(y_sample, *layer_args)
    return (y_prompt, y_sample)
```

```python
import numpy as np
from contextlib import ExitStack
import concourse.bass as bass
import concourse.mybir as mybir
from concourse.bass_utils import run_bass_kernel_spmd

F32 = mybir.dt.float32
BF16 = mybir.dt.bfloat16
ALU = mybir.AluOpType
AF = mybir.ActivationFunctionType

NCORES = 8
D = 1024
DFF = 2816
EPS = 1e-6
NEG = -30000.0
SAME_ENGINE_SYNC = True

_s = np.exp2(-8.0 * np.arange(1, 17, dtype=np.float32) / 16).astype(np.float32)
SLOPES_A = [float(v) for v in _s[8:]]
SLOPES_B = [float(v) for v in _s[:8]]

SEGS = [
    dict(name="p0", nrows=3328, t0=0, t1=26, k_lo=127, k_hi=3201, q_lo=1151, q_hi=2177, o_lo=1152, o_hi=2176),
    dict(name="p1", nrows=3328, t0=0, t1=26, k_lo=127, k_hi=3201, q_lo=1151, q_hi=2177, o_lo=1152, o_hi=2176),
    dict(name="a", nrows=2304, t0=1, t1=17, k_lo=128, k_hi=2176, q_lo=128, q_hi=2176, o_lo=128, o_hi=2176),
    dict(name="b", nrows=2304, t0=1, t1=17, k_lo=128, k_hi=2176, q_lo=128, q_hi=2176, o_lo=128, o_hi=2176),
]
NTOK_MAX = 3328
NACC = 2050
NKT = 320
FBLK = 342


def cdiv(a, b):
    return -(-a // b)


class Prog:
    ENGS = ("pe", "act", "dve", "pool", "sp")

    def __init__(self):
        self.ops = []

    def op(self, eng, fn, reads=(), writes=(), dma=False, fence=False):
        reads = tuple(reads) + (() if fence else ("PHASE",))
        writes = tuple(writes) + (("PHASE",) if fence else ())
        self.ops.append(dict(eng=eng, fn=fn, reads=reads, writes=writes, dma=dma, fence=fence))

    def schedule(self):
        ops = self.ops
        last_w = {}
        readers = {}
        for i, o in enumerate(ops):
            deps = set()
            for k in o["reads"]:
                if k in last_w:
                    deps.add(last_w[k])
            for k in o["writes"]:
                if k in last_w:
                    deps.add(last_w[k])
                for r in readers.get(k, ()):
                    deps.add(r)
            deps.discard(i)
            if o["fence"]:
                deps = set(d for d in deps if not (ops[d]["dma"] and ops[d].get("consumed")))
            for d in deps:
                if ops[d]["dma"]:
                    ops[d]["consumed"] = True
            o["deps"] = deps
            for k in o["writes"]:
                last_w[k] = i
                readers[k] = []
            for k in o["reads"]:
                readers.setdefault(k, []).append(i)
        for i, o in enumerate(ops):
            best = {}
            keep = []
            for d in o["deps"]:
                od = ops[d]
                if od["dma"]:
                    keep.append(d)
                    continue
                e = od["eng"]
                if e == o["eng"] and not o["dma"]:
                    if e == "pe" or not SAME_ENGINE_SYNC:
                        continue
                if e not in best or d > best[e]:
                    best[e] = d
            o["deps"] = keep + list(best.values())
        signalled = set()
        for o in ops:
            for d in o["deps"]:
                signalled.add(d)
        self.signalled = signalled
        cnt = {e: 0 for e in self.ENGS}
        for i, o in enumerate(ops):
            if o["dma"]:
                continue
            if i in signalled:
                cnt[o["eng"]] += 1
                o["rank"] = cnt[o["eng"]]
        self.NDS = 24
        dcnt = {}
        slot_hist = {}
        for i, o in enumerate(ops):
            if not o["dma"]:
                continue
            q = o["eng"]
            n = dcnt.get(q, 0)
            dcnt[q] = n + 1
            slot = n % self.NDS
            gen = n // self.NDS + 1
            o["dsem"] = (q, slot)
            o["dval"] = 16 * gen
            o["prev_same_slot"] = slot_hist.get((q, slot))
            slot_hist[(q, slot)] = i
        self.final_dma = slot_hist


def emit_program(nc, prog, sems, dsems):
    ops = prog.ops
    by_eng = {e: [] for e in Prog.ENGS}
    for i, o in enumerate(ops):
        by_eng[o["eng"]].append(i)

    def run_engine(ename, eh):
        waited = {}

        def wait(sem_key, sem, val):
            if waited.get(sem_key, 0) >= val:
                return
            waited[sem_key] = val
            eh.wait_ge(sem, val)

        for i in by_eng[ename]:
            o = ops[i]
            if o["dma"] and o["prev_same_slot"] is not None:
                p = ops[o["prev_same_slot"]]
                wait(("d",) + p["dsem"], dsems[p["dsem"]], p["dval"])
            for d in o["deps"]:
                od = ops[d]
                if od["dma"]:
                    wait(("d",) + od["dsem"], dsems[od["dsem"]], od["dval"])
                else:
                    wait(od["eng"], sems[od["eng"]], od["rank"])
            ins = o["fn"](eh)
            if o["dma"]:
                ins.then_inc(dsems[o["dsem"]], 16)
            elif i in prog.signalled:
                ins.then_inc(sems[ename], 1)
        if ename == "sp":
            for key, i in prog.final_dma.items():
                o = ops[i]
                wait(("d",) + o["dsem"], dsems[o["dsem"]], o["dval"])

    return run_engine


def build_program():
    nc = bass.Bass("TRN2", target_bir_lowering=False)
    P = Prog()
    kt_tab = []

    def din(name, shape):
        return nc.dram_tensor(name, list(shape), F32, kind="ExternalInput").ap()

    x_in = [din("x0", (3328, D)), din("x1", (3328, D)), din("x2", (2304, D)), din("x3", (2304, D))]
    w_in = din("w_in", (D, 2304))
    w_out = din("w_out", (D, D))
    w_up = din("w_up", (D, 2 * DFF))
    w_down = din("w_down", (DFF, D))
    c_ident = din("c_ident", (128, 128))
    c_blk = din("c_blk", (128, 128))
    c_da = din("c_da", (128, 256))
    c_db = din("c_db", (128, 384))
    c_pvec = din("c_pvec", (128, 16))
    c_conv = din("c_conv", (128, 4 * 44))
    c_n1 = din("c_n1", (1, D))
    c_n2 = din("c_n2", (1, D))
    c_sink = din("c_sink", (1, 8))
    c_vb = din("c_vb", (128, NKT))
    c_qv = din("c_qv", (128, 16))
    c_sel = din("c_sel", (128, 256))
    c_m2 = din("c_m2", (128, 8))
    out_p = nc.dram_tensor("out_p", [2048, D], F32, kind="ExternalOutput").ap()
    out_s = nc.dram_tensor("out_s", [4096, D], F32, kind="ExternalOutput").ap()

    wi_s = nc.dram_tensor("wi_s", [18, 128, 1024], BF16).ap()
    wo_s = nc.dram_tensor("wo_s", [128, 8 * D], BF16).ap()
    wd_s = nc.dram_tensor("wd_s", [128, 22 * D], BF16).ap()
    wu_s = nc.dram_tensor("wu_s", [22, 128, 2048], BF16).ap()
    SCR = dict(wi_s=wi_s, wo_s=wo_s, wd_s=wd_s, wu_s=wu_s)

    es = ExitStack()

    def sb(name, shape, dt):
        return es.enter_context(nc.sbuf_tensor(name, list(shape), dt))

    def ps(name, shape, dt):
        return es.enter_context(nc.psum_tensor(name, list(shape), dt))

    cst2 = sb("cst2", (128, 256), F32)
    identf = cst2[:, 0:128]
    identb = sb("identb", (128, 128), BF16)
    blkf = cst2[:, 128:256]
    blkb = sb("blkb", (128, 128), BF16)
    onesb = sb("onesb", (128, 128), BF16)
    DA = sb("DA", (128, 256), F32)
    DB = sb("DB", (128, 384), F32)
    pvec = sb("pvec", (128, 16), F32)
    convp = sb("convp", (128, 176), F32)
    n1bc = sb("n1bc", (128, D), F32)
    n2bc = sb("n2bc", (128, D), F32)
    sinkc = sb("sinkc", (128, 8), F32)
    vb = sb("vb", (128, NKT), F32)
    epsc = sb("epsc", (128, 1), F32)
    qv = sb("qv", (128, 16), F32)
    sel = cst2
    m2 = sb("m2", (128, 8), F32)
    yT = sb("yT", (128, 8, NACC), BF16)
    xt = [sb("xt0", (128, D), F32), sb("xt1", (128, D), F32)]
    hb = [sb("hb0", (128, D), BF16), sb("hb1", (128, D), BF16)]
    st_ss = sb("st_ss", (128, 8), F32)
    junkb = sb("junkb", (128, D), BF16)
    NB_A = 8 * NTOK_MAX + 2048 + 2 * NTOK_MAX + 58 * 192 + 8 * 384 + 3 * 1024 + 2 * 512
    NB_F = 8 * D + 22 * D + 8 * 512 + 22 * 512 + 3 * 2048
    NF_A = 2 * NACC + 3 * 1024 + 4 * 512 + 1024
    NF_F = 4 * D + 9 * 512 + 2 * D
    AB = sb("arena_b", (128, max(NB_A, NB_F)), BF16)
    AFa = sb("arena_f", (128, max(NF_A, NF_F)), F32)

    class Carver:
        def __init__(self, t):
            self.t, self.o = t, 0

        def take(self, n, c=None):
            v = self.t[:, self.o:self.o + n]
            self.o += n
            if c is not None:
                v = v.rearrange("p (c t) -> p c t", c=c)
            return v
    cb_, cf_ = Carver(AB), Carver(AFa)
    hT = cb_.take(8 * NTOK_MAX, 8)
    wsl = [cb_.take(1024, 8) for i in range(3)]
    VT = cb_.take(NTOK_MAX)
    QT = cb_.take(2048)
    KT = cb_.take(NTOK_MAX)
    NVX = 58
    Vx = cb_.take(NVX * 192, NVX)
    NSB = 8
    PT = [cb_.take(384) for i in range(NSB)]
    sqb = [cb_.take(512) for i in range(2)]
    accn = cf_.take(NACC)
    accd = cf_.take(NACC)
    wsl_f = [cf_.take(1024, 8) for i in range(3)]
    tmpf = [cf_.take(512) for i in range(2)]
    rsf = [cf_.take(512) for i in range(2)]
    xt_a = cf_.take(1024)
    cb_, cf_ = Carver(AB), Carver(AFa)
    FB = dict(
        wd=cb_.take(22 * D, 22), wo=cb_.take(8 * D, 8), h2T=cb_.take(8 * 512, 8), gT=cb_.take(22 * 512, 22),
        wu=[cb_.take(2048, 8) for i in range(3)],
        x1=cf_.take(4 * D, 4),
        cg=[cf_.take(512) for i in range(3)], cv=[cf_.take(512) for i in range(3)], sgt=[cf_.take(512) for i in range(3)],
        ot=[cf_.take(D) for i in range(2)],
    )
    FB["SCR"] = SCR
    fence_t = sb("fence_t", (128, 2), F32)

    def fence():
        P.op("dve", lambda e: e.memset(fence_t[:], 0.0), [], [], fence=True)

    PS_S = ps("PS_S", (128, 1024), F32)
    PS_N = ps("PS_N", (128, 1024), F32)
    PS_D = ps("PS_D", (128, 1024), F32)
    PS_X = ps("PS_X", (128, 512), F32)
    PS_T = ps("PS_T", (128, 512), F32)
    bankS = [PS_S[:, 0:512], PS_S[:, 512:1024], PS_X[:, 0:512], PS_T[:, 0:512]]
    bankN = [PS_N[:, 0:512], PS_N[:, 512:1024]]
    bankD = [PS_D[:, 0:512], PS_D[:, 512:1024]]
    bankT = [PS_T[:, 0:512].bitcast(BF16), PS_X[:, 0:512].bitcast(BF16)]
    TKEY = ["S3", "S2"]
    FB_T = TKEY

    def dma(q, out, in_, reads, writes):
        P.op(q, lambda e, out=out, in_=in_: e.dma_start(out=out, in_=in_), reads, writes, dma=True)

    def act(out, in_, func, reads, writes, scale=1.0, bias=None, accum=None):
        def fn(e, out=out, in_=in_, func=func, scale=scale, bias=bias, accum=accum):
            kw = {}
            if bias is not None:
                kw["bias"] = bias
            if accum is not None:
                kw["accum_out"] = accum
            return e.activation(out=out, in_=in_, func=func, scale=scale, **kw)
        P.op("act", fn, reads, writes)

    def dve(fn, reads, writes):
        P.op("dve", fn, reads, writes)

    def pool(fn, reads, writes):
        P.op("pool", fn, reads, writes)

    def mm(out, lhsT, rhs, start, stop, reads, writes, sgc=False):
        def fn(e, out=out, lhsT=lhsT, rhs=rhs, start=start, stop=stop, sgc=sgc):
            return e.matmul(out, lhsT=lhsT, rhs=rhs, start=start, stop=stop, skip_group_check=sgc)
        P.op("pe", fn, reads, writes)

    def tr(out, in_, ident, reads, writes):
        P.op("pe", lambda e, out=out, in_=in_, ident=ident: e.transpose(out, in_, ident), reads, writes)

    def rstd_chain(ss_ap, out_ap, tmp_ap, scale, reads_key, tmp_key, out_key):
        act(tmp_ap, ss_ap, AF.Ln, [reads_key, "epsc"], [tmp_key], scale=scale, bias=epsc[0:ss_ap.shape[0], 0:1])
        act(out_ap, tmp_ap, AF.Exp, [tmp_key], [out_key], scale=-0.5)

    dma("sp", identf, c_ident, [], ["cst2"])
    dma("sp", blkf, c_blk, [], ["cst2"])
    dma("sp", DA[:], c_da, [], ["DA"])
    dma("sp", DB[:], c_db, [], ["DB"])
    dma("sp", pvec[:], c_pvec, [], ["pvec"])
    dma("sp", convp[:], c_conv, [], ["convp"])
    dma("sp", n1bc[:], c_n1.partition_broadcast(128), [], ["n1bc"])
    dma("sp", n2bc[:], c_n2.partition_broadcast(128), [], ["n2bc"])
    dma("sp", sinkc[:], c_sink.partition_broadcast(128), [], ["sinkc"])
    dma("sp", vb[:], c_vb, [], ["vb"])
    dma("sp", qv[:], c_qv, [], ["qv"])
    FB["qv"] = qv
    dve(lambda e: e.tensor_copy(out=identb[:], in_=identf), ["cst2"], ["identb"])
    dve(lambda e: e.tensor_copy(out=blkb[:], in_=blkf), ["cst2"], ["blkb"])
    dve(lambda e: e.memset(onesb[:], 1.0), [], ["onesb"])
    dve(lambda e: e.memset(epsc[:], EPS), [], ["epsc"])
    act(sinkc[:], sinkc[:], AF.Exp, ["sinkc"], ["sinkc"])
    dma("sp", sel[:], c_sel, [], ["cst2", "sel"])
    dma("sp", m2[:], c_m2, [], ["m2"])
    dve(lambda e: e.tensor_tensor(out=sinkc[:], in0=sinkc[:], in1=m2[:], op=ALU.mult), ["sinkc", "m2"], ["sinkc"])

    state = dict(xslot=0, sbslot=0, vxslot=0, tslot=0, qslot=0, nslot=0, sslot=0)

    for si, sg in enumerate(SEGS):
        xin = x_in[si]
        q_lo, q_hi, k_lo, k_hi, o_lo, o_hi = sg["q_lo"], sg["q_hi"], sg["k_lo"], sg["k_hi"], sg["o_lo"], sg["o_hi"]
        a0 = o_lo - 1

        def stage1_tile(ti):
            xs = ti % 3
            xtile = (xt + [xt_a])[xs]
            par = ti % 2
            c3 = 3 * par
            dma("sp", xtile[:], xin[ti * 128:(ti + 1) * 128, :], [], [f"xt{xs}"])
            act(junkb[:], xtile[:], AF.Square, [f"xt{xs}"], [f"ss{c3}"], accum=st_ss[:, c3:c3 + 1])
            rstd_chain(st_ss[:, c3:c3 + 1], st_ss[:, c3 + 2:c3 + 3], st_ss[:, c3 + 1:c3 + 2], 1.0 / D, f"ss{c3}", f"ss{c3 + 1}", f"ss{c3 + 2}")
            dve(lambda e, xtile=xtile, par=par, c3=c3: e.scalar_tensor_tensor(out=hb[par][:], in0=xtile[:], scalar=st_ss[:, c3 + 2:c3 + 3], in1=n1bc[:],
                                                                            op0=ALU.mult, op1=ALU.mult),
                [f"xt{xs}", f"ss{c3 + 2}", "n1bc"], [f"hb{par}"])
            for kc in range(8):
                tr(bankT[par][:, kc * 128:(kc + 1) * 128], hb[par][:, kc * 128:(kc + 1) * 128], identb[:],
                   [f"hb{par}", "identb"], [TKEY[par]])

        def stage1_copy(ti):
            par = ti % 2
            src = bankT[par].rearrange("p (c t) -> p c t", c=8)
            if ti % 2 == 0:
                act(hT[:, :, ti * 128:(ti + 1) * 128], src, AF.Copy, [TKEY[par]], [f"hT{ti}"])
            else:
                dve(lambda e, src=src, ti=ti: e.tensor_copy(out=hT[:, :, ti * 128:(ti + 1) * 128], in_=src),
                    [TKEY[par]], [f"hT{ti}"])

        dve(lambda e: e.memset(Vx[:, :, 64:128], 1.0), [], [f"Vx{v}" for v in range(NVX)])

        def hT_keys(s0, s1):
            return [f"hT{t}" for t in range(s0 // 128, cdiv(s1, 128))]

        def blk_keys(name, s0, s1):
            return [f"{name}{b}" for b in range(s0 // 512, cdiv(s1, 512))]

        if si >= 2:
            dve(lambda e: e.memset(yT[:, :, 0:1], 0.0), [], [f"yT{c}" for c in range(8)])
            dve(lambda e: e.memset(yT[:, :, NACC - 1:NACC], 0.0), [], [f"yT{c}" for c in range(8)])

        def load_w(slot, col0, ncols_list):
            if si == 0:
                for (dc, sc, n) in ncols_list:
                    dma("sp", wsl_f[slot][:, :, dc:dc + n],
                        w_in.rearrange("(c p) n -> p c n", p=128)[:, :, sc:sc + n], [], [f"wslf{slot}"])
                dve(lambda e, slot=slot: e.tensor_copy(out=wsl[slot][:], in_=wsl_f[slot][:]), [f"wslf{slot}"], [f"wsl{slot}"])
                (dc, sc, n) = ncols_list[0]
                b, off = sc // 128, sc % 128
                dma("pool", wi_s[b].rearrange("p (c n) -> p c n", c=8)[:, :, off:off + n], wsl[slot][:, :, 0:n],
                    [f"wsl{slot}"], [f"wi_s{b}_{off}"])
            else:
                for (dc, sc, n) in ncols_list:
                    b, off = sc // 128, sc % 128
                    dma("sp", wsl[slot][:, :, dc:dc + n], wi_s[b].rearrange("p (c n) -> p c n", c=8)[:, :, off:off + n],
                        [f"wi_s{b}_{off}"], [f"wsl{slot}"])

        def project_all(jobs, with_stage1=False):
            blocks = []
            for (slot, dst, dst_name, s_lo, s_hi, gain_col, korg) in jobs:
                s = s_lo
                while s < s_hi:
                    n = min(512, s_hi - s)
                    blocks.append((slot, dst, dst_name, s, n, gain_col, korg))
                    s += n
            if with_stage1:
                pass
            info = {}
            st1 = dict(next=sg["t0"])

            def stage1_upto(col_end):
                while st1["next"] < sg["t1"] and (st1["next"] - 2) * 128 < col_end:
                    stage1_tile(st1["next"])
                    if st1["next"] > sg["t0"]:
                        stage1_copy(st1["next"] - 1)
                    st1["next"] += 1
                if st1["next"] == sg["t1"] and not st1.get("done"):
                    stage1_copy(sg["t1"] - 1)
                    st1["done"] = True

            def head(i):
                (slot, dst, dst_name, s, n, gain_col, korg) = blocks[i]
                bs = state["sslot"]; state["sslot"] = (bs + 1) % 4
                pb = bankS[bs]
                for kc in range(8):
                    mm(pb[:, 0:n], wsl[slot][:, kc, :], hT[:, kc, s:s + n], kc == 0, kc == 7,
                       [f"wsl{slot}"] + hT_keys(s, s + n), [f"S{bs}"])
                wk = blk_keys(dst_name, s - korg, s + n - korg)
                qs_ = i % 2
                if gain_col is None:
                    act(dst[:, s:s + n], pb[:, 0:n], AF.Copy, [f"S{bs}"], wk)
                else:
                    act(sqb[qs_][:, 0:n], pb[:, 0:n], AF.Square, [f"S{bs}"], [f"sqb{qs_}"])
                info[i] = (bs, qs_, wk)

            def tail(i):
                (slot, dst, dst_name, s, n, gain_col, korg) = blocks[i]
                if gain_col is None:
                    return
                (bs, qs_, wk) = info[i]
                pb = bankS[bs]
                mm(bankN[qs_][:, 0:n], blkb[:], sqb[qs_][:, 0:n], True, True, ["blkb", f"sqb{qs_}"], [f"N{qs_}"])
                rstd_chain(bankN[qs_][:, 0:n], rsf[qs_][:, 0:n], tmpf[qs_][:, 0:n], 1.0 / 64, f"N{qs_}", f"tmpf{qs_}", f"rsf{qs_}")
                dve(lambda e, pb=pb, n=n, s=s, qs_=qs_, gain_col=gain_col, dst=dst:
                    e.scalar_tensor_tensor(out=dst[:, s:s + n], in0=pb[:, 0:n], scalar=pvec[:, gain_col:gain_col + 1],
                                           in1=rsf[qs_][:, 0:n], op0=ALU.mult, op1=ALU.mult),
                    [f"S{bs}", "pvec", f"rsf{qs_}"], wk)

            for i in range(len(blocks) + 1):
                if i < len(blocks):
                    if with_stage1:
                        stage1_upto(10 ** 9)
                    head(i)
                if i >= 1:
                    tail(i - 1)
            if with_stage1:
                stage1_upto(10 ** 9)

        def attention(R, configs, Dm, Dm_key, slopes2, kv_lo, kv_hi, sink_cols, after_vx=None, reuse_vx=False):
            LA = 4
            WB = [bankN[0], bankN[1], bankD[0], bankD[1]]
            WBK = ["N0", "N1", "D0", "D1"]
            classes = []
            vx_jobs = []
            for d in configs:
                kk_lo = max(k_lo, q_lo - R * d, kv_lo)
                kk_hi = min(k_hi, q_hi + R * d, kv_hi)
                for r in range(d):
                    jq_lo, jq_hi = cdiv(q_lo - r, d), cdiv(q_hi - r, d)
                    jk_lo, jk_hi = cdiv(kk_lo - r, d), cdiv(kk_hi - r, d)
                    if jq_hi <= jq_lo or jk_hi <= jk_lo:
                        continue
                    tinfo = []
                    a = jk_lo
                    while a < jk_hi:
                        n = min(128, jk_hi - a)
                        key = (si, d, r, a, n)
                        if key not in kt_index:
                            kt_index[key] = len(kt_tab)
                            kt_tab.append((si, [r + d * (a + i) for i in range(n)]))
                        kt = kt_index[key]
                        s0 = r + d * a
                        s1 = r + d * (a + n - 1) + 1
                        vslot = len(vx_jobs)
                        vx_jobs.append((vslot, s0, s1, d, n))
                        tinfo.append((a, n, kt, vslot, s0, s1))
                        a += n
                    classes.append((d, r, jq_lo, jq_hi, tinfo))
            assert len(vx_jobs) <= NVX, len(vx_jobs)
            if not reuse_vx:
                dve(lambda e: e.memset(bankS[3][:, :], 0.0), [], ["S3"])
                dve(lambda e: e.memset(bankS[2][:, :], 0.0), [], ["S2"])
            for g0 in range(0, 0 if reuse_vx else len(vx_jobs), 8):
                grp = vx_jobs[g0:g0 + 8]
                tb = (g0 // 8) % 2
                tkey = TKEY[tb]
                for j, (vslot, s0, s1, d, n) in enumerate(grp):
                    tr(bankT[tb][0:n, j * 128:(j + 1) * 128], VT[:, s0:s1:d], identb[:], blk_keys("VT", s0, s1) + ["identb"], [tkey])
                ng = len(grp)
                src = bankT[tb][:, 0:ng * 128].rearrange("p (g h c) -> p g h c", g=ng, h=2)
                dstv = Vx[:, g0:g0 + ng, :].rearrange("p g (h c) -> p g h c", h=3)[:, :, 0:3:2, :]
                wkeys = [f"Vx{v}" for (v, _, _, _, _) in grp]
                if (g0 // 8) % 2 == 0:
                    dve(lambda e, src=src, dstv=dstv: e.tensor_copy(out=dstv, in_=src), [tkey], wkeys)
                else:
                    act(dstv, src, AF.Copy, [tkey], wkeys)
            if after_vx is not None:
                after_vx()
            units = []
            first_cfg_d = configs[0]
            widx = 0
            for (d, r, jq_lo, jq_hi, tinfo) in classes:
                w_lo = jq_lo
                while w_lo < jq_hi:
                    w_hi = min(jq_hi, w_lo + 512)
                    geo = []
                    for (a, n, kt, vslot, s0, s1) in tinfo:
                        qs = max(w_lo, a - R)
                        qe = min(w_hi, a + n + R)
                        if qs < qe:
                            geo.append((a, n, kt, vslot, s0, s1, qs, qe))
                    assert geo
                    for gi_, (a, n, kt, vslot, s0, s1, qs, qe) in enumerate(geo):
                        for hf in range(2):
                            units.append(dict(hf=hf, rows=slice(64 * hf, 64 * hf + 64), d=d, r=r, scal=8.0 * slopes2[hf] * d,
                                              a=a, n=n, kt=kt, vslot=vslot, s0=s0, s1=s1, qs=qs, qe=qe, w_lo=w_lo, w_hi=w_hi,
                                              ns=2 * (widx % 2) + hf, first=(gi_ == 0), last=(gi_ == len(geo) - 1)))
                    widx += 1
                    w_lo = w_hi

            def u_head(i, u):
                n, nq = u["n"], u["qe"] - u["qs"]
                d, r, rows = u["d"], u["r"], u["rows"]
                c0 = u["qs"] - (u["a"] - R)
                sq0 = r + d * u["qs"] - q_lo
                sq1 = r + d * (u["qe"] - 1) - q_lo + 1
                bs = state["sslot"]; state["sslot"] = (bs + 1) % 4
                ps_ = state["sbslot"]; state["sbslot"] = (ps_ + 1) % NSB
                u["ps"] = ps_
                pb = bankS[bs]
                mm(pb[0:n, 0:nq], KT[rows, u["s0"]:u["s1"]:d], QT[rows, sq0:sq1:d], True, True,
                   blk_keys("KT", u["s0"], u["s1"]) + blk_keys("QT", sq0, sq1), [f"S{bs}"])
                dve(lambda e, pb=pb, n=n, nq=nq, c0=c0, scal=u["scal"]:
                    e.scalar_tensor_tensor(out=pb[0:n, 0:nq], in0=Dm[0:n, c0:c0 + nq], scalar=scal,
                                           in1=pb[0:n, 0:nq], op0=ALU.mult, op1=ALU.add),
                    [Dm_key, f"S{bs}"], [f"S{bs}"])
                act(PT[ps_][0:n, 0:nq], pb[0:n, 0:nq], AF.Exp, [f"S{bs}", "vb"], [f"PT{ps_}"],
                    scale=0.125, bias=vb[0:n, u["kt"]:u["kt"] + 1])

            def u_tail(i, u):
                n, nq = u["n"], u["qe"] - u["qs"]
                d, r, ns, hf = u["d"], u["r"], u["ns"], u["hf"]
                ps_ = u["ps"]
                o0 = u["qs"] - u["w_lo"]
                wb = WB[ns]
                wkey = WBK[ns]
                mm(wb[:, o0:o0 + nq], Vx[0:n, u["vslot"], 64 * hf:64 * hf + 128], PT[ps_][0:n, 0:nq], u["first"], u["last"],
                   [f"Vx{u['vslot']}", f"PT{ps_}"], [wkey], sgc=True)
                if not u["last"]:
                    return
                w_lo, w_hi = u["w_lo"], u["w_hi"]
                nw = w_hi - w_lo
                c_lo = r + d * w_lo - a0
                c_hi = r + d * (w_hi - 1) - a0 + 1
                acc = accn if hf == 0 else accd
                akey = f"acc{hf}"
                if d == first_cfg_d:
                    if sink_cols is None:
                        act(acc[:, c_lo:c_hi:d], wb[:, 0:nw], AF.Copy, [wkey], [akey])
                    else:
                        sc_ = sink_cols[hf]
                        act(acc[:, c_lo:c_hi:d], wb[:, 0:nw], AF.Identity, [wkey, "sinkc"], [akey], bias=sinkc[:, sc_:sc_ + 1])
                else:
                    dve(lambda e, acc=acc, c_lo=c_lo, c_hi=c_hi, d=d, wb=wb, nw=nw:
                        e.tensor_tensor(out=acc[:, c_lo:c_hi:d], in0=wb[:, 0:nw], in1=acc[:, c_lo:c_hi:d], op=ALU.add),
                        [wkey, akey], [akey])

            for i in range(0, len(units) + LA, 2):
                for j in (i, i + 1):
                    if j < len(units):
                        u_head(j, units[j])
                for j in (i - LA, i + 1 - LA):
                    if 0 <= j < len(units):
                        u_tail(j, units[j])

        def normalize_pair(chunk):
            c = q_lo - a0
            c_end = q_hi - a0
            while c < c_end:
                n = min(512, c_end - c)
                t_ = state["qslot"]; state["qslot"] ^= 1
                mm(bankN[t_][:, 0:n], sel[:, 0:128], accn[:, c:c + n], True, False, ["sel", "acc0"], [f"N{t_}"])
                mm(bankN[t_][:, 0:n], sel[:, 128:256], accd[:, c:c + n], False, True, ["sel", "acc1"], [f"N{t_}"])
                dve(lambda e, c=c, n=n, t_=t_: e.tensor_scalar(out=tmpf[t_][:, 0:n], in0=bankN[t_][:, 0:n], scalar1=1e-30, scalar2=None, op0=ALU.max),
                    [f"N{t_}"], [f"tmpf{t_}"])
                act(tmpf[t_][:, 0:n], tmpf[t_][:, 0:n], AF.Ln, [f"tmpf{t_}"], [f"tmpf{t_}"])
                act(rsf[t_][:, 0:n], tmpf[t_][:, 0:n], AF.Exp, [f"tmpf{t_}"], [f"rsf{t_}"], scale=-1.0)
                dve(lambda e, c=c, n=n, t_=t_, chunk=chunk: e.tensor_tensor(out=yT[0:64, chunk, c:c + n], in0=accn[0:64, c:c + n], in1=rsf[t_][0:64, 0:n], op=ALU.mult),
                    ["acc0", f"rsf{t_}"], [f"yT{chunk}"])
                dve(lambda e, c=c, n=n, t_=t_, chunk=chunk: e.tensor_tensor(out=yT[64:128, chunk, c:c + n], in0=accd[64:128, c:c + n], in1=rsf[t_][64:128, 0:n], op=ALU.mult),
                    ["acc1", f"rsf{t_}"], [f"yT{chunk}"])
                c += n

        def out_norm(m):
            c = 0
            ncol = o_hi - o_lo + 2
            while c < ncol:
                n = min(256, ncol - c)
                t_ = state["qslot"]; state["qslot"] ^= 1
                sq = [sqb[j // 2][:, (j % 2) * 256:(j % 2) * 256 + n] for j in range(4)]
                for j in range(4):
                    ch = 4 * m + j
                    act(sq[j], yT[:, ch, c:c + n], AF.Square, [f"yT{ch}"], [f"sqb{j // 2}"])
                for j in range(4):
                    mm(bankN[t_][:, 0:n], onesb[:], sq[j], j == 0, j == 3, ["onesb", f"sqb{j // 2}"], [f"N{t_}"])
                rstd_chain(bankN[t_][:, 0:n], rsf[t_][:, 0:n], tmpf[t_][:, 0:n], 1.0 / 512, f"N{t_}", f"tmpf{t_}", f"rsf{t_}")
                for j in range(4):
                    ch = 4 * m + j
                    dve(lambda e, ch=ch, c=c, n=n, t_=t_: e.scalar_tensor_tensor(out=yT[:, ch, c:c + n], in0=yT[:, ch, c:c + n],
                                                                                scalar=pvec[:, 4 + ch:5 + ch], in1=rsf[t_][:, 0:n],
                                                                                op0=ALU.mult, op1=ALU.mult),
                        [f"yT{ch}", "pvec", f"rsf{t_}"], [f"yT{ch}"])
                c += n

        kt_index = build_program.kt_index
        for p in range(4):
            load_w(0, 0, [(0, 128 * p, 128)])
            load_w(1, 0, [(0, 512 + 128 * p, 128)])
            load_w(2, 0, [(0, 1024 + 128 * p, 128)])
            project_all([(2, VT, "VT", k_lo, k_hi, None, 0), (0, QTv(QT, q_lo), "QT", q_lo, q_hi, 0, q_lo),
                         (1, KT, "KT", k_lo, k_hi, 1, 0)], with_stage1=(p == 0))
            attention(64, (1, 4, 16), DA, "DA", (SLOPES_A[2 * p], SLOPES_A[2 * p + 1]), k_lo, k_hi, None)
            normalize_pair(p)
        kb_lo, kb_hi = max(k_lo, q_lo - 128), min(k_hi, q_hi + 128)
        for p in range(4):
            g = p // 2
            load_w(0, 0, [(0, 1536 + 128 * p, 128)])
            if p % 2 == 0:
                load_w(1, 0, [(0, 2048 + 64 * g, 64), (64, 2048 + 64 * g, 64)])
                load_w(2, 0, [(0, 2176 + 64 * g, 64), (64, 2176 + 64 * g, 64)])
                project_all([(2, VT, "VT", kb_lo, kb_hi, None, 0), (0, QTv(QT, q_lo), "QT", q_lo, q_hi, 2, q_lo),
                             (1, KT, "KT", kb_lo, kb_hi, 3, 0)])
            else:
                project_all([(0, QTv(QT, q_lo), "QT", q_lo, q_hi, 2, q_lo)])
            def prefetch_ffn_w():
                dead = [f"hT{t}" for t in range(NTOK_MAX // 128)] + ["wsl0", "wsl1", "wsl2"] + [f"VT{b}" for b in range(cdiv(NTOK_MAX, 512))]
                for j in range(2):
                    dma("sp", FB["wd"][:, 11 * j:11 * (j + 1), :].rearrange("p c n -> p (c n)"),
                        SCR["wd_s"][:, 11 * j * D:11 * (j + 1) * D], ["wd_s"], ["wd"] + dead)
                dma("sp", FB["wo"].rearrange("p c n -> p (c n)"), SCR["wo_s"], ["wo_s"], ["wo"] + dead)
            attention(128, (1,), DB, "DB", (SLOPES_B[2 * p], SLOPES_B[2 * p + 1]), kb_lo, kb_hi, (2 * p, 2 * p + 1),
                      after_vx=(prefetch_ffn_w if (p == 3 and si > 0) else None), reuse_vx=(p % 2 == 1))
            normalize_pair(4 + p)
            if p == 0:
                out_norm(0)
        out_norm(1)

        fence()
        ffn_phase(nc, P, es, sg, si, xin, yT, xt, hb, st_ss, junkb, n2bc, convp, identb, epsc,
                  w_out, w_up, w_down, out_p if si < 2 else out_s, (si * 1024 if si < 2 else (si - 2) * 2048),
                  bankS, bankN, bankD, bankT, state, act, dve, pool, mm, tr, dma, rstd_chain, FB)
        fence()

    return nc, P, es, kt_tab


class QTv:
    def __init__(self, t, q_lo):
        self.t, self.q_lo = t, q_lo

    def __getitem__(self, key):
        rows, cols = key
        return self.t[rows, cols.start - self.q_lo:cols.stop - self.q_lo]


build_program.kt_index = {}
QV_TAB = []


def ffn_phase(nc, P, es, sg, si, xin, yT, xt, hb, st_ss, junkb, n2bc, convp, identb, epsc,
              w_out, w_up, w_down, out_ap, out_row0, bankS, bankN, bankD, bankT, state,
              act, dve, pool, mm, tr, dma, rstd_chain, B):
    o_lo, o_hi = sg["o_lo"], sg["o_hi"]
    a0 = o_lo - 1
    wo, wd, x1, h2T, gT = B["wo"], B["wd"], B["x1"], B["h2T"], B["gT"]
    SCR = B["SCR"]
    if si == 0:
        k = 0
        for kc in range(8):
            s_ = k % 2; k += 1
            dma("sp", B["ot"][s_][:], w_out[kc * 128:(kc + 1) * 128, :], [], [f"ot{s_}"])
            dve(lambda e, s_=s_, kc=kc: e.tensor_copy(out=wo[:, kc, :], in_=B["ot"][s_][:]), [f"ot{s_}"], ["wo"])
        dma("pool", SCR["wo_s"], wo.rearrange("p c n -> p (c n)"), ["wo"], ["wo_s"])
        for fc in range(22):
            s_ = k % 2; k += 1
            dma("sp", B["ot"][s_][:], w_down[fc * 128:(fc + 1) * 128, :], [], [f"ot{s_}"])
            dve(lambda e, s_=s_, fc=fc: e.tensor_copy(out=wd[:, fc, :], in_=B["ot"][s_][:]), [f"ot{s_}"], ["wd"])
        dma("pool", SCR["wd_s"], wd.rearrange("p c n -> p (c n)"), ["wd"], ["wd_s"])
    else:
        pass

    yT_keys = [f"yT{c}" for c in range(8)]
    wur = w_up.rearrange("(c p) n -> p c n", p=128)
    PAIRS = [(bankS, "S"), (bankN, "N"), (bankD, "D")]

    blocks = []
    b0 = o_lo
    while b0 < o_hi:
        s_first = b0 - 1
        n = min(FBLK + 2, o_hi + 1 - s_first)
        tiles = []
        c = 0
        while c < n:
            m = min(128, n - c)
            tiles.append((c, m))
            c += m
        blocks.append(dict(b0=b0, s_first=s_first, n=n, tiles=tiles))
        b0 += FBLK

    def s4_tail(bi, ti, hs):
        (c, m) = blocks[bi]["tiles"][ti]
        tsl = state["tslot"]; state["tslot"] ^= 1
        tk = ["S3", "S2"][tsl]
        for kc in range(8):
            tr(bankT[tsl][:, kc * 128:kc * 128 + m], hb[hs][0:m, kc * 128:(kc + 1) * 128], identb[0:m, 0:m],
               [f"hb{hs}", "identb"], [tk])
        src = bankT[tsl].rearrange("p (c t) -> p c t", c=8)[:, :, 0:m]
        act(h2T[:, :, c:c + m], src, AF.Copy, [tk], ["h2T"])

    def s4_tile(bi, ti):
        blk = blocks[bi]
        (c, m) = blk["tiles"][ti]
        s_first, n, tiles = blk["s_first"], blk["n"], blk["tiles"]
        xs = state["xslot"]; state["xslot"] ^= 1
        xtile = xt[xs]
        dma("sp", xtile[0:m, :], xin[s_first + c:s_first + c + m, :], [], [f"xt{xs}"])
        pp = PAIRS[0]
        yc0 = s_first + c - a0
        for half in range(2):
            for kc in range(8):
                mm(pp[0][half][0:m, :], yT[:, kc, yc0:yc0 + m], wo[:, kc, half * 512:(half + 1) * 512], kc == 0, kc == 7,
                   yT_keys + ["wo"], [f"{pp[1]}{half}"])
        for half in range(2):
            dve(lambda e, pp=pp, half=half, m=m, ti=ti, xtile=xtile:
                e.tensor_tensor(out=x1[0:m, ti, half * 512:(half + 1) * 512], in0=pp[0][half][0:m, :],
                                in1=xtile[0:m, half * 512:(half + 1) * 512], op=ALU.add),
                [f"{pp[1]}{half}", f"xt{xs}"], [f"x1_{ti}"])
        c3 = 3 * (ti % 2)
        hs = ti % 2
        act(junkb[0:m, :], x1[0:m, ti, :], AF.Square, [f"x1_{ti}"], [f"ss{c3}"], accum=st_ss[0:m, c3:c3 + 1])
        rstd_chain(st_ss[0:m, c3:c3 + 1], st_ss[0:m, c3 + 2:c3 + 3], st_ss[0:m, c3 + 1:c3 + 2], 1.0 / D, f"ss{c3}", f"ss{c3 + 1}", f"ss{c3 + 2}")
        specials = []
        if ti == 0 and blk["b0"] == o_lo:
            specials.append((0, o_lo - 1))
        if ti == len(tiles) - 1 and s_first + n == o_hi + 1:
            specials.append((m - 1, o_hi))
        for (row, stok) in specials:
            col = len(QV_TAB)
            QV_TAB.append((si, row, stok))
            dve(lambda e, m=m, col=col, c3=c3: e.tensor_tensor(out=st_ss[0:m, c3 + 2:c3 + 3], in0=st_ss[0:m, c3 + 2:c3 + 3], in1=B["qv"][0:m, col:col + 1], op=ALU.mult),
                [f"ss{c3 + 2}", "qv"], [f"ss{c3 + 2}"])
        dve(lambda e, m=m, ti=ti, hs=hs, c3=c3: e.scalar_tensor_tensor(out=hb[hs][0:m, :], in0=x1[0:m, ti, :], scalar=st_ss[0:m, c3 + 2:c3 + 3], in1=n2bc[0:m, :],
                                                          op0=ALU.mult, op1=ALU.mult),
            [f"x1_{ti}", f"ss{c3 + 2}", "n2bc"], [f"hb{hs}"])
        return (bi, ti, hs)

    def u_phase(bi):
        n = blocks[bi]["n"]
        gkeys = [f"gT{f_}" for f_ in range(22)]
        pool(lambda e: e.memset(gT[:, :, 0:1], 0.0), [], gkeys)
        pool(lambda e, n=n: e.memset(gT[:, :, n - 1:n], 0.0), [], gkeys)
        for fc in range(22):
            ws = fc % 3
            if si == 0 and bi == 0:
                wf = B["ot"]
                for gi in range(2):
                    dma("sp", wf[gi].rearrange("p (c n) -> p c n", c=8), wur[:, :, gi * DFF + fc * 128:gi * DFF + (fc + 1) * 128],
                        [], [f"ot{gi}"])
                    (dve if gi == 0 else pool)(lambda e, ws=ws, gi=gi, wf=wf: e.tensor_copy(out=B["wu"][ws][:, :, gi * 128:(gi + 1) * 128],
                                                                   in_=wf[gi].rearrange("p (c n) -> p c n", c=8)),
                        [f"ot{gi}"], [f"wu{ws}"])
                dma("pool", SCR["wu_s"][fc], B["wu"][ws].rearrange("p c n -> p (c n)"), [f"wu{ws}"], [f"wu_s{fc}"])
            else:
                dma("sp", B["wu"][ws].rearrange("p c n -> p (c n)"), SCR["wu_s"][fc], [f"wu_s{fc}"], [f"wu{ws}"])
            pp = PAIRS[fc % 3]
            cs = fc % 3
            for gi in range(2):
                pb = pp[0][gi]
                pk = f"{pp[1]}{gi}"
                for kc in range(8):
                    mm(pb[:, 0:n], B["wu"][ws][:, kc, gi * 128:(gi + 1) * 128], h2T[:, kc, 0:n], kc == 0, kc == 7,
                       [f"wu{ws}", "h2T"], [pk])
                cbuf = (B["cg"] if gi == 0 else B["cv"])[cs]
                ckey = ("cg" if gi == 0 else "cv") + str(cs)
                fidx = fc + 22 * gi
                act(cbuf[:, 1:n - 1], pb[:, 1:n - 1], AF.Identity, [pk, "convp"], [ckey],
                    scale=convp[:, 44 + fidx:45 + fidx], bias=convp[:, 132 + fidx:133 + fidx])
                dve(lambda e, cbuf=cbuf, pb=pb, n=n, fidx=fidx: e.scalar_tensor_tensor(
                    out=cbuf[:, 1:n - 1], in0=pb[:, 0:n - 2], scalar=convp[:, fidx:fidx + 1], in1=cbuf[:, 1:n - 1],
                    op0=ALU.mult, op1=ALU.add), [pk, "convp", ckey], [ckey])
                dve(lambda e, cbuf=cbuf, pb=pb, n=n, fidx=fidx: e.scalar_tensor_tensor(
                    out=cbuf[:, 1:n - 1], in0=pb[:, 2:n], scalar=convp[:, 88 + fidx:89 + fidx], in1=cbuf[:, 1:n - 1],
                    op0=ALU.mult, op1=ALU.add), [pk, "convp", ckey], [ckey])
            act(B["sgt"][cs][:, 1:n - 1], B["cg"][cs][:, 1:n - 1], AF.Silu, [f"cg{cs}"], [f"sgt{cs}"])
            pool(lambda e, cs=cs, fc=fc, n=n: e.tensor_tensor(out=gT[:, fc, 1:n - 1], in0=B["sgt"][cs][:, 1:n - 1],
                                                            in1=B["cv"][cs][:, 1:n - 1], op=ALU.mult),
                 [f"sgt{cs}", f"cv{cs}"], [f"gT{fc}"])

    def d_tile(bi, ti):
        blk = blocks[bi]
        (c, m) = blk["tiles"][ti]
        s_first, tiles = blk["s_first"], blk["tiles"]
        pp = PAIRS[1 + ti % 2]
        for half in range(2):
            for fc in range(22):
                mm(pp[0][half][0:m, :], gT[:, fc, c:c + m], wd[:, fc, half * 512:(half + 1) * 512], fc == 0, fc == 21,
                   [f"gT{fc}", "wd"], [f"{pp[1]}{half}"])
        os_ = ti % 2
        for half in range(2):
            dve(lambda e, pp=pp, half=half, m=m, ti=ti, os_=os_:
                e.tensor_tensor(out=B["ot"][os_][0:m, half * 512:(half + 1) * 512], in0=pp[0][half][0:m, :],
                                in1=x1[0:m, ti, half * 512:(half + 1) * 512], op=ALU.add),
                [f"{pp[1]}{half}", f"x1_{ti}"], [f"ot{os_}"])
        r0 = 1 if ti == 0 else 0
        r1 = m - 1 if ti == len(tiles) - 1 else m
        if r1 > r0:
            tok0 = s_first + c + r0 - o_lo
            dma("pool", out_ap[out_row0 + tok0:out_row0 + tok0 + (r1 - r0), :], B["ot"][os_][r0:r1, :], [f"ot{os_}"], [])

    pending = None
    for ti in range(len(blocks[0]["tiles"])):
        nxt = s4_tile(0, ti)
        if pending is not None:
            s4_tail(*pending)
        pending = nxt
    s4_tail(*pending)
    for bi in range(len(blocks)):
        u_phase(bi)
        nt = len(blocks[bi]["tiles"])
        nt_next = len(blocks[bi + 1]["tiles"]) if bi + 1 < len(blocks) else 0
        pending = None
        for ti in range(max(nt, nt_next)):
            if ti < nt:
                d_tile(bi, ti)
            if ti < nt_next:
                nxt = s4_tile(bi + 1, ti)
                if pending is not None:
                    s4_tail(*pending)
                pending = nxt
        if pending is not None:
            s4_tail(*pending)


def build_all():
    nc, P, es, kt_tab = build_program()
    P.schedule()
    with ExitStack() as es2:
        sems = {e: es2.enter_context(nc.semaphore(f"sem_{e}")) for e in Prog.ENGS}
        dsems = {}
        for q in ("sp", "pool"):
            for i in range(P.NDS):
                dsems[(q, i)] = es2.enter_context(nc.semaphore(f"dsem_{q}_{i}"))
        block = es2.enter_context(nc.Block())
        run = emit_program(nc, P, sems, dsems)

        @block.sync
        def _(e):
            run("sp", e)

        @block.tensor
        def _(e):
            run("pe", e)

        @block.scalar
        def _(e):
            run("act", e)

        @block.vector
        def _(e):
            run("dve", e)

        @block.gpsimd
        def _(e):
            run("pool", e)
    es.close()
    return nc, kt_tab


_cache = {}


def kernel(x_prompt, x_sample, norm1, w_in, q_norm_a, k_norm_a, q_norm_b, k_norm_b, sink_b,
           out_norm_a, out_norm_b, w_out, norm2, w_up, conv_w, conv_b, w_down):
    if "nc" not in _cache:
        _cache["nc"] = build_all()
    nc, kt_tab = _cache["nc"]
    f = np.float32
    x_prompt = np.asarray(x_prompt, f)
    x_sample = np.asarray(x_sample, f)
    ident = np.eye(128, dtype=f)
    blk = np.zeros((128, 128), f)
    blk[:64, :64] = 1
    blk[64:, 64:] = 1
    i = np.arange(128)[:, None]
    c = np.arange(256)[None, :]
    dist = np.abs(i - c + 64)
    da = np.where(dist <= 64, -dist, -1e5).astype(f)
    c = np.arange(384)[None, :]
    dist = np.abs(i - c + 128)
    db = np.where(dist <= 128, -dist, -1e5).astype(f)
    pvec = np.zeros((128, 16), f)
    pvec[:, 0] = np.tile(np.asarray(q_norm_a, f)[0], 2)
    pvec[:, 1] = np.tile(np.asarray(k_norm_a, f)[0], 2)
    pvec[:, 2] = np.tile(np.asarray(q_norm_b, f)[0], 2)
    pvec[:, 3] = np.tile(np.asarray(k_norm_b, f)[0], 2)
    pvec[:, 4:8] = np.asarray(out_norm_a, f)[0].reshape(4, 128).T
    pvec[:, 8:12] = np.asarray(out_norm_b, f)[0].reshape(4, 128).T
    cw = np.asarray(conv_w, f)[0]
    cb = np.asarray(conv_b, f)[0]
    convp = np.concatenate([cw[0].reshape(44, 128).T, cw[1].reshape(44, 128).T, cw[2].reshape(44, 128).T,
                            cb.reshape(44, 128).T], axis=1).astype(f)
    common = dict(w_in=np.ascontiguousarray(np.asarray(w_in, f)[0]), w_out=np.ascontiguousarray(np.asarray(w_out, f)[0]),
                  w_up=np.ascontiguousarray(np.asarray(w_up, f)[0]), w_down=np.ascontiguousarray(np.asarray(w_down, f)[0]),
                  c_ident=ident, c_blk=blk, c_da=da, c_db=db, c_pvec=pvec, c_conv=np.ascontiguousarray(convp),
                  c_n1=np.asarray(norm1, f).reshape(1, D), c_n2=np.asarray(norm2, f).reshape(1, D),
                  c_sink=np.asarray(sink_b, f).reshape(1, 8))
    selm = np.zeros((128, 256), f)
    for j in range(64):
        selm[64 + j, j] = 1.0
        selm[j, 128 + 64 + j] = 1.0
    m2m = np.zeros((128, 8), f)
    for h in range(8):
        if h % 2 == 0:
            m2m[64:, h] = 1.0
        else:
            m2m[:64, h] = 1.0
    common.update(c_sel=selm, c_m2=m2m)
    in_maps = []
    for core in range(NCORES):
        xps, tlos = [], []
        for h in range(2):
            x0 = np.zeros((3328, D), f)
            t_lo = 2048 * core + 1024 * h - 1152
            lo, hi = max(0, t_lo), min(16384, t_lo + 3328)
            x0[lo - t_lo:hi - t_lo] = x_prompt[0, lo:hi]
            xps.append(x0)
            tlos.append(t_lo)
        xs = []
        for j in range(2):
            xx = np.zeros((2304, D), f)
            xx[128:2176] = x_sample[2 * core + j]
            xs.append(xx)
        vbt = np.zeros((128, NKT), f)
        assert len(kt_tab) <= NKT, len(kt_tab)
        for col, (si, pos) in enumerate(kt_tab):
            if si < 2:
                t = tlos[si] + np.asarray(pos)
                vbt[:len(pos), col] = np.where((t >= 0) & (t < 16384), 0.0, NEG)
        qvt = np.ones((128, 16), f)
        assert len(QV_TAB) <= 16
        for col, (si, row, stok) in enumerate(QV_TAB):
            if si < 2:
                t = tlos[si] + stok
                qvt[row, col] = 1.0 if (0 <= t < 16384) else 0.0
            else:
                qvt[row, col] = 0.0
        m = dict(common)
        m.update(x0=xps[0], x1=xps[1], x2=xs[0], x3=xs[1], c_vb=vbt, c_qv=qvt)
        in_maps.append(m)
    res = run_bass_kernel_spmd(nc, in_maps, core_ids=list(range(NCORES)))
    y_p = np.concatenate([r["out_p"] for r in res.results], axis=0).reshape(1, 16384, D)
    y_s = np.concatenate([r["out_s"].reshape(2, 2048, D) for r in res.results], axis=0)
    return (y_p.astype(f), y_s.astype(f))
```

```python
import numpy as np
from contextlib import ExitStack
import concourse.bass as bass
import concourse.mybir as mybir
from concourse.bass_utils import run_bass_kernel_spmd

F32 = mybir.dt.float32
BF16 = mybir.dt.bfloat16
ALU = mybir.AluOpType
AF = mybir.ActivationFunctionType

NCORES = 8
D = 1024
DFF = 2816
EPS = 1e-6
NEG = -30000.0
SAME_ENGINE_SYNC = True

_s = np.exp2(-8.0 * np.arange(1, 17, dtype=np.float32) / 16).astype(np.float32)
SLOPES_A = [float(v) for v in _s[8:]]
SLOPES_B = [float(v) for v in _s[:8]]

SEGS = [
    dict(name="p0", nrows=3328, t0=0, t1=26, k_lo=127, k_hi=3201, q_lo=1151, q_hi=2177, o_lo=1152, o_hi=2176),
    dict(name="p1", nrows=3328, t0=0, t1=26, k_lo=127, k_hi=3201, q_lo=1151, q_hi=2177, o_lo=1152, o_hi=2176),
    dict(name="a", nrows=2304, t0=1, t1=17, k_lo=128, k_hi=2176, q_lo=128, q_hi=2176, o_lo=128, o_hi=2176),
    dict(name="b", nrows=2304, t0=1, t1=17, k_lo=128, k_hi=2176, q_lo=128, q_hi=2176, o_lo=128, o_hi=2176),
]
NTOK_MAX = 3328
NACC = 2050
NKT = 320
FBLK = 342


def cdiv(a, b):
    return -(-a // b)


class Prog:
    ENGS = ("pe", "act", "dve", "pool", "sp")

    def __init__(self):
        self.ops = []

    def op(self, eng, fn, reads=(), writes=(), dma=False, fence=False):
        reads = tuple(reads) + (() if fence else ("PHASE",))
        writes = tuple(writes) + (("PHASE",) if fence else ())
        self.ops.append(dict(eng=eng, fn=fn, reads=reads, writes=writes, dma=dma, fence=fence))

    def schedule(self):
        ops = self.ops
        last_w = {}
        readers = {}
        for i, o in enumerate(ops):
            deps = set()
            for k in o["reads"]:
                if k in last_w:
                    deps.add(last_w[k])
            for k in o["writes"]:
                if k in last_w:
                    deps.add(last_w[k])
                for r in readers.get(k, ()):
                    deps.add(r)
            deps.discard(i)
            if o["fence"]:
                deps = set(d for d in deps if not (ops[d]["dma"] and ops[d].get("consumed")))
            for d in deps:
                if ops[d]["dma"]:
                    ops[d]["consumed"] = True
            o["deps"] = deps
            for k in o["writes"]:
                last_w[k] = i
                readers[k] = []
            for k in o["reads"]:
                readers.setdefault(k, []).append(i)
        for i, o in enumerate(ops):
            best = {}
            keep = []
            for d in o["deps"]:
                od = ops[d]
                if od["dma"]:
                    keep.append(d)
                    continue
                e = od["eng"]
                if e == o["eng"] and not o["dma"]:
                    if e == "pe" or not SAME_ENGINE_SYNC:
                        continue
                if e not in best or d > best[e]:
                    best[e] = d
            o["deps"] = keep + list(best.values())
        signalled = set()
        for o in ops:
            for d in o["deps"]:
                signalled.add(d)
        self.signalled = signalled
        cnt = {e: 0 for e in self.ENGS}
        for i, o in enumerate(ops):
            if o["dma"]:
                continue
            if i in signalled:
                cnt[o["eng"]] += 1
                o["rank"] = cnt[o["eng"]]
        self.NDS = 24
        dcnt = {}
        slot_hist = {}
        for i, o in enumerate(ops):
            if not o["dma"]:
                continue
            q = o["eng"]
            n = dcnt.get(q, 0)
            dcnt[q] = n + 1
            slot = n % self.NDS
            gen = n // self.NDS + 1
            o["dsem"] = (q, slot)
            o["dval"] = 16 * gen
            o["prev_same_slot"] = slot_hist.get((q, slot))
            slot_hist[(q, slot)] = i
        self.final_dma = slot_hist


def emit_program(nc, prog, sems, dsems):
    ops = prog.ops
    by_eng = {e: [] for e in Prog.ENGS}
    for i, o in enumerate(ops):
        by_eng[o["eng"]].append(i)

    def run_engine(ename, eh):
        waited = {}

        def wait(sem_key, sem, val):
            if waited.get(sem_key, 0) >= val:
                return
            waited[sem_key] = val
            eh.wait_ge(sem, val)

        for i in by_eng[ename]:
            o = ops[i]
            if o["dma"] and o["prev_same_slot"] is not None:
                p = ops[o["prev_same_slot"]]
                wait(("d",) + p["dsem"], dsems[p["dsem"]], p["dval"])
            for d in o["deps"]:
                od = ops[d]
                if od["dma"]:
                    wait(("d",) + od["dsem"], dsems[od["dsem"]], od["dval"])
                else:
                    wait(od["eng"], sems[od["eng"]], od["rank"])
            ins = o["fn"](eh)
            if o["dma"]:
                ins.then_inc(dsems[o["dsem"]], 16)
            elif i in prog.signalled:
                ins.then_inc(sems[ename], 1)
        if ename == "sp":
            for key, i in prog.final_dma.items():
                o = ops[i]
                wait(("d",) + o["dsem"], dsems[o["dsem"]], o["dval"])

    return run_engine


def build_program():
    nc = bass.Bass("TRN2", target_bir_lowering=False)
    P = Prog()
    kt_tab = []

    def din(name, shape):
        return nc.dram_tensor(name, list(shape), F32, kind="ExternalInput").ap()

    x_in = [din("x0", (3328, D)), din("x1", (3328, D)), din("x2", (2304, D)), din("x3", (2304, D))]
    w_in = din("w_in", (D, 2304))
    w_out = din("w_out", (D, D))
    w_up = din("w_up", (D, 2 * DFF))
    w_down = din("w_down", (DFF, D))
    c_ident = din("c_ident", (128, 128))
    c_blk = din("c_blk", (128, 128))
    c_da = din("c_da", (128, 256))
    c_db = din("c_db", (128, 384))
    c_pvec = din("c_pvec", (128, 16))
    c_conv = din("c_conv", (128, 4 * 44))
    c_n1 = din("c_n1", (1, D))
    c_n2 = din("c_n2", (1, D))
    c_sink = din("c_sink", (1, 8))
    c_vb = din("c_vb", (128, NKT))
    c_qv = din("c_qv", (128, 16))
    c_sel = din("c_sel", (128, 256))
    c_m2 = din("c_m2", (128, 8))
    out_p = nc.dram_tensor("out_p", [2048, D], F32, kind="ExternalOutput").ap()
    out_s = nc.dram_tensor("out_s", [4096, D], F32, kind="ExternalOutput").ap()

    wi_s = nc.dram_tensor("wi_s", [18, 128, 1024], BF16).ap()
    wo_s = nc.dram_tensor("wo_s", [128, 8 * D], BF16).ap()
    wd_s = nc.dram_tensor("wd_s", [128, 22 * D], BF16).ap()
    wu_s = nc.dram_tensor("wu_s", [22, 128, 2048], BF16).ap()
    SCR = dict(wi_s=wi_s, wo_s=wo_s, wd_s=wd_s, wu_s=wu_s)

    es = ExitStack()

    def sb(name, shape, dt):
        return es.enter_context(nc.sbuf_tensor(name, list(shape), dt))

    def ps(name, shape, dt):
        return es.enter_context(nc.psum_tensor(name, list(shape), dt))

    cst2 = sb("cst2", (128, 256), F32)
    identf = cst2[:, 0:128]
    identb = sb("identb", (128, 128), BF16)
    blkf = cst2[:, 128:256]
    blkb = sb("blkb", (128, 128), BF16)
    onesb = sb("onesb", (128, 128), BF16)
    DA = sb("DA", (128, 256), F32)
    DB = sb("DB", (128, 384), F32)
    pvec = sb("pvec", (128, 16), F32)
    convp = sb("convp", (128, 176), F32)
    n1bc = sb("n1bc", (128, D), F32)
    n2bc = sb("n2bc", (128, D), F32)
    sinkc = sb("sinkc", (128, 8), F32)
    vb = sb("vb", (128, NKT), F32)
    epsc = sb("epsc", (128, 1), F32)
    qv = sb("qv", (128, 16), F32)
    sel = cst2
    m2 = sb("m2", (128, 8), F32)
    yT = sb("yT", (128, 8, NACC), BF16)
    xt = [sb("xt0", (128, D), F32), sb("xt1", (128, D), F32)]
    hb = [sb("hb0", (128, D), BF16), sb("hb1", (128, D), BF16)]
    st_ss = sb("st_ss", (128, 8), F32)
    junkb = sb("junkb", (128, D), BF16)
    NB_A = 8 * NTOK_MAX + 2048 + 2 * NTOK_MAX + 58 * 192 + 8 * 384 + 3 * 1024 + 2 * 512
    NB_F = 8 * D + 22 * D + 8 * 512 + 22 * 512 + 3 * 2048
    NF_A = 2 * NACC + 3 * 1024 + 4 * 512 + 1024
    NF_F = 4 * D + 9 * 512 + 2 * D
    AB = sb("arena_b", (128, max(NB_A, NB_F)), BF16)
    AFa = sb("arena_f", (128, max(NF_A, NF_F)), F32)

    class Carver:
        def __init__(self, t):
            self.t, self.o = t, 0

        def take(self, n, c=None):
            v = self.t[:, self.o:self.o + n]
            self.o += n
            if c is not None:
                v = v.rearrange("p (c t) -> p c t", c=c)
            return v
    cb_, cf_ = Carver(AB), Carver(AFa)
    hT = cb_.take(8 * NTOK_MAX, 8)
    wsl = [cb_.take(1024, 8) for i in range(3)]
    VT = cb_.take(NTOK_MAX)
    QT = cb_.take(2048)
    KT = cb_.take(NTOK_MAX)
    NVX = 58
    Vx = cb_.take(NVX * 192, NVX)
    NSB = 8
    PT = [cb_.take(384) for i in range(NSB)]
    sqb = [cb_.take(512) for i in range(2)]
    accn = cf_.take(NACC)
    accd = cf_.take(NACC)
    wsl_f = [cf_.take(1024, 8) for i in range(3)]
    tmpf = [cf_.take(512) for i in range(2)]
    rsf = [cf_.take(512) for i in range(2)]
    xt_a = cf_.take(1024)
    cb_, cf_ = Carver(AB), Carver(AFa)
    FB = dict(
        wd=cb_.take(22 * D, 22), wo=cb_.take(8 * D, 8), h2T=cb_.take(8 * 512, 8), gT=cb_.take(22 * 512, 22),
        wu=[cb_.take(2048, 8) for i in range(3)],
        x1=cf_.take(4 * D, 4),
        cg=[cf_.take(512) for i in range(3)], cv=[cf_.take(512) for i in range(3)], sgt=[cf_.take(512) for i in range(3)],
        ot=[cf_.take(D) for i in range(2)],
    )
    FB["SCR"] = SCR
    fence_t = sb("fence_t", (128, 2), F32)

    def fence():
        P.op("dve", lambda e: e.memset(fence_t[:], 0.0), [], [], fence=True)

    PS_S = ps("PS_S", (128, 1024), F32)
    PS_N = ps("PS_N", (128, 1024), F32)
    PS_D = ps("PS_D", (128, 1024), F32)
    PS_X = ps("PS_X", (128, 512), F32)
    PS_T = ps("PS_T", (128, 512), F32)
    bankS = [PS_S[:, 0:512], PS_S[:, 512:1024], PS_X[:, 0:512], PS_T[:, 0:512]]
    bankN = [PS_N[:, 0:512], PS_N[:, 512:1024]]
    bankD = [PS_D[:, 0:512], PS_D[:, 512:1024]]
    bankT = [PS_T[:, 0:512].bitcast(BF16), PS_X[:, 0:512].bitcast(BF16)]
    TKEY = ["S3", "S2"]
    FB_T = TKEY

    def dma(q, out, in_, reads, writes):
        P.op(q, lambda e, out=out, in_=in_: e.dma_start(out=out, in_=in_), reads, writes, dma=True)

    def act(out, in_, func, reads, writes, scale=1.0, bias=None, accum=None):
        def fn(e, out=out, in_=in_, func=func, scale=scale, bias=bias, accum=accum):
            kw = {}
            if bias is not None:
                kw["bias"] = bias
            if accum is not None:
                kw["accum_out"] = accum
            return e.activation(out=out, in_=in_, func=func, scale=scale, **kw)
        P.op("act", fn, reads, writes)

    def dve(fn, reads, writes):
        P.op("dve", fn, reads, writes)

    def pool(fn, reads, writes):
        P.op("pool", fn, reads, writes)

    def mm(out, lhsT, rhs, start, stop, reads, writes, sgc=False):
        def fn(e, out=out, lhsT=lhsT, rhs=rhs, start=start, stop=stop, sgc=sgc):
            return e.matmul(out, lhsT=lhsT, rhs=rhs, start=start, stop=stop, skip_group_check=sgc)
        P.op("pe", fn, reads, writes)

    def tr(out, in_, ident, reads, writes):
        P.op("pe", lambda e, out=out, in_=in_, ident=ident: e.transpose(out, in_, ident), reads, writes)

    def rstd_chain(ss_ap, out_ap, tmp_ap, scale, reads_key, tmp_key, out_key):
        act(tmp_ap, ss_ap, AF.Ln, [reads_key, "epsc"], [tmp_key], scale=scale, bias=epsc[0:ss_ap.shape[0], 0:1])
        act(out_ap, tmp_ap, AF.Exp, [tmp_key], [out_key], scale=-0.5)

    dma("sp", identf, c_ident, [], ["cst2"])
    dma("sp", blkf, c_blk, [], ["cst2"])
    dma("sp", DA[:], c_da, [], ["DA"])
    dma("sp", DB[:], c_db, [], ["DB"])
    dma("sp", pvec[:], c_pvec, [], ["pvec"])
    dma("sp", convp[:], c_conv, [], ["convp"])
    dma("sp", n1bc[:], c_n1.partition_broadcast(128), [], ["n1bc"])
    dma("sp", n2bc[:], c_n2.partition_broadcast(128), [], ["n2bc"])
    dma("sp", sinkc[:], c_sink.partition_broadcast(128), [], ["sinkc"])
    dma("sp", vb[:], c_vb, [], ["vb"])
    dma("sp", qv[:], c_qv, [], ["qv"])
    FB["qv"] = qv
    dve(lambda e: e.tensor_copy(out=identb[:], in_=identf), ["cst2"], ["identb"])
    dve(lambda e: e.tensor_copy(out=blkb[:], in_=blkf), ["cst2"], ["blkb"])
    dve(lambda e: e.memset(onesb[:], 1.0), [], ["onesb"])
    dve(lambda e: e.memset(epsc[:], EPS), [], ["epsc"])
    act(sinkc[:], sinkc[:], AF.Exp, ["sinkc"], ["sinkc"])
    dma("sp", sel[:], c_sel, [], ["cst2", "sel"])
    dma("sp", m2[:], c_m2, [], ["m2"])
    dve(lambda e: e.tensor_tensor(out=sinkc[:], in0=sinkc[:], in1=m2[:], op=ALU.mult), ["sinkc", "m2"], ["sinkc"])

    state = dict(xslot=0, sbslot=0, vxslot=0, tslot=0, qslot=0, nslot=0, sslot=0)

    for si, sg in enumerate(SEGS):
        xin = x_in[si]
        q_lo, q_hi, k_lo, k_hi, o_lo, o_hi = sg["q_lo"], sg["q_hi"], sg["k_lo"], sg["k_hi"], sg["o_lo"], sg["o_hi"]
        a0 = o_lo - 1

        def stage1_tile(ti):
            xs = ti % 3
            xtile = (xt + [xt_a])[xs]
            par = ti % 2
            c3 = 3 * par
            dma("sp", xtile[:], xin[ti * 128:(ti + 1) * 128, :], [], [f"xt{xs}"])
            act(junkb[:], xtile[:], AF.Square, [f"xt{xs}"], [f"ss{c3}"], accum=st_ss[:, c3:c3 + 1])
            rstd_chain(st_ss[:, c3:c3 + 1], st_ss[:, c3 + 2:c3 + 3], st_ss[:, c3 + 1:c3 + 2], 1.0 / D, f"ss{c3}", f"ss{c3 + 1}", f"ss{c3 + 2}")
            dve(lambda e, xtile=xtile, par=par, c3=c3: e.scalar_tensor_tensor(out=hb[par][:], in0=xtile[:], scalar=st_ss[:, c3 + 2:c3 + 3], in1=n1bc[:],
                                                                            op0=ALU.mult, op1=ALU.mult),
                [f"xt{xs}", f"ss{c3 + 2}", "n1bc"], [f"hb{par}"])
            for kc in range(8):
                tr(bankT[par][:, kc * 128:(kc + 1) * 128], hb[par][:, kc * 128:(kc + 1) * 128], identb[:],
                   [f"hb{par}", "identb"], [TKEY[par]])

        def stage1_copy(ti):
            par = ti % 2
            src = bankT[par].rearrange("p (c t) -> p c t", c=8)
            if ti % 2 == 0:
                act(hT[:, :, ti * 128:(ti + 1) * 128], src, AF.Copy, [TKEY[par]], [f"hT{ti}"])
            else:
                dve(lambda e, src=src, ti=ti: e.tensor_copy(out=hT[:, :, ti * 128:(ti + 1) * 128], in_=src),
                    [TKEY[par]], [f"hT{ti}"])

        dve(lambda e: e.memset(Vx[:, :, 64:128], 1.0), [], [f"Vx{v}" for v in range(NVX)])

        def hT_keys(s0, s1):
            return [f"hT{t}" for t in range(s0 // 128, cdiv(s1, 128))]

        def blk_keys(name, s0, s1):
            return [f"{name}{b}" for b in range(s0 // 512, cdiv(s1, 512))]

        if si >= 2:
            dve(lambda e: e.memset(yT[:, :, 0:1], 0.0), [], [f"yT{c}" for c in range(8)])
            dve(lambda e: e.memset(yT[:, :, NACC - 1:NACC], 0.0), [], [f"yT{c}" for c in range(8)])

        def load_w(slot, col0, ncols_list):
            if si == 0:
                for (dc, sc, n) in ncols_list:
                    dma("sp", wsl_f[slot][:, :, dc:dc + n],
                        w_in.rearrange("(c p) n -> p c n", p=128)[:, :, sc:sc + n], [], [f"wslf{slot}"])
                dve(lambda e, slot=slot: e.tensor_copy(out=wsl[slot][:], in_=wsl_f[slot][:]), [f"wslf{slot}"], [f"wsl{slot}"])
                (dc, sc, n) = ncols_list[0]
                b, off = sc // 128, sc % 128
                dma("pool", wi_s[b].rearrange("p (c n) -> p c n", c=8)[:, :, off:off + n], wsl[slot][:, :, 0:n],
                    [f"wsl{slot}"], [f"wi_s{b}_{off}"])
            else:
                for (dc, sc, n) in ncols_list:
                    b, off = sc // 128, sc % 128
                    dma("sp", wsl[slot][:, :, dc:dc + n], wi_s[b].rearrange("p (c n) -> p c n", c=8)[:, :, off:off + n],
                        [f"wi_s{b}_{off}"], [f"wsl{slot}"])

        def project_all(jobs, with_stage1=False):
            blocks = []
            for (slot, dst, dst_name, s_lo, s_hi, gain_col, korg) in jobs:
                s = s_lo
                while s < s_hi:
                    n = min(512, s_hi - s)
                    blocks.append((slot, dst, dst_name, s, n, gain_col, korg))
                    s += n
            if with_stage1:
                pass
            info = {}
            st1 = dict(next=sg["t0"])

            def stage1_upto(col_end):
                while st1["next"] < sg["t1"] and (st1["next"] - 2) * 128 < col_end:
                    stage1_tile(st1["next"])
                    if st1["next"] > sg["t0"]:
                        stage1_copy(st1["next"] - 1)
                    st1["next"] += 1
                if st1["next"] == sg["t1"] and not st1.get("done"):
                    stage1_copy(sg["t1"] - 1)
                    st1["done"] = True

            def head(i):
                (slot, dst, dst_name, s, n, gain_col, korg) = blocks[i]
                bs = state["sslot"]; state["sslot"] = (bs + 1) % 4
                pb = bankS[bs]
                for kc in range(8):
                    mm(pb[:, 0:n], wsl[slot][:, kc, :], hT[:, kc, s:s + n], kc == 0, kc == 7,
                       [f"wsl{slot}"] + hT_keys(s, s + n), [f"S{bs}"])
                wk = blk_keys(dst_name, s - korg, s + n - korg)
                qs_ = i % 2
                if gain_col is None:
                    act(dst[:, s:s + n], pb[:, 0:n], AF.Copy, [f"S{bs}"], wk)
                else:
                    act(sqb[qs_][:, 0:n], pb[:, 0:n], AF.Square, [f"S{bs}"], [f"sqb{qs_}"])
                info[i] = (bs, qs_, wk)

            def tail(i):
                (slot, dst, dst_name, s, n, gain_col, korg) = blocks[i]
                if gain_col is None:
                    return
                (bs, qs_, wk) = info[i]
                pb = bankS[bs]
                mm(bankN[qs_][:, 0:n], blkb[:], sqb[qs_][:, 0:n], True, True, ["blkb", f"sqb{qs_}"], [f"N{qs_}"])
                rstd_chain(bankN[qs_][:, 0:n], rsf[qs_][:, 0:n], tmpf[qs_][:, 0:n], 1.0 / 64, f"N{qs_}", f"tmpf{qs_}", f"rsf{qs_}")
                dve(lambda e, pb=pb, n=n, s=s, qs_=qs_, gain_col=gain_col, dst=dst:
                    e.scalar_tensor_tensor(out=dst[:, s:s + n], in0=pb[:, 0:n], scalar=pvec[:, gain_col:gain_col + 1],
                                           in1=rsf[qs_][:, 0:n], op0=ALU.mult, op1=ALU.mult),
                    [f"S{bs}", "pvec", f"rsf{qs_}"], wk)

            for i in range(len(blocks) + 1):
                if i < len(blocks):
                    if with_stage1:
                        stage1_upto(10 ** 9)
                    head(i)
                if i >= 1:
                    tail(i - 1)
            if with_stage1:
                stage1_upto(10 ** 9)

        def attention(R, configs, Dm, Dm_key, slopes2, kv_lo, kv_hi, sink_cols, after_vx=None, reuse_vx=False):
            LA = 4
            WB = [bankN[0], bankN[1], bankD[0], bankD[1]]
            WBK = ["N0", "N1", "D0", "D1"]
            classes = []
            vx_jobs = []
            for d in configs:
                kk_lo = max(k_lo, q_lo - R * d, kv_lo)
                kk_hi = min(k_hi, q_hi + R * d, kv_hi)
                for r in range(d):
                    jq_lo, jq_hi = cdiv(q_lo - r, d), cdiv(q_hi - r, d)
                    jk_lo, jk_hi = cdiv(kk_lo - r, d), cdiv(kk_hi - r, d)
                    if jq_hi <= jq_lo or jk_hi <= jk_lo:
                        continue
                    tinfo = []
                    a = jk_lo
                    while a < jk_hi:
                        n = min(128, jk_hi - a)
                        key = (si, d, r, a, n)
                        if key not in kt_index:
                            kt_index[key] = len(kt_tab)
                            kt_tab.append((si, [r + d * (a + i) for i in range(n)]))
                        kt = kt_index[key]
                        s0 = r + d * a
                        s1 = r + d * (a + n - 1) + 1
                        vslot = len(vx_jobs)
                        vx_jobs.append((vslot, s0, s1, d, n))
                        tinfo.append((a, n, kt, vslot, s0, s1))
                        a += n
                    classes.append((d, r, jq_lo, jq_hi, tinfo))
            assert len(vx_jobs) <= NVX, len(vx_jobs)
            if not reuse_vx:
                dve(lambda e: e.memset(bankS[3][:, :], 0.0), [], ["S3"])
                dve(lambda e: e.memset(bankS[2][:, :], 0.0), [], ["S2"])
            for g0 in range(0, 0 if reuse_vx else len(vx_jobs), 8):
                grp = vx_jobs[g0:g0 + 8]
                tb = (g0 // 8) % 2
                tkey = TKEY[tb]
                for j, (vslot, s0, s1, d, n) in enumerate(grp):
                    tr(bankT[tb][0:n, j * 128:(j + 1) * 128], VT[:, s0:s1:d], identb[:], blk_keys("VT", s0, s1) + ["identb"], [tkey])
                ng = len(grp)
                src = bankT[tb][:, 0:ng * 128].rearrange("p (g h c) -> p g h c", g=ng, h=2)
                dstv = Vx[:, g0:g0 + ng, :].rearrange("p g (h c) -> p g h c", h=3)[:, :, 0:3:2, :]
                wkeys = [f"Vx{v}" for (v, _, _, _, _) in grp]
                if (g0 // 8) % 2 == 0:
                    dve(lambda e, src=src, dstv=dstv: e.tensor_copy(out=dstv, in_=src), [tkey], wkeys)
                else:
                    act(dstv, src, AF.Copy, [tkey], wkeys)
            if after_vx is not None:
                after_vx()
            units = []
            first_cfg_d = configs[0]
            widx = 0
            for (d, r, jq_lo, jq_hi, tinfo) in classes:
                w_lo = jq_lo
                while w_lo < jq_hi:
                    w_hi = min(jq_hi, w_lo + 512)
                    geo = []
                    for (a, n, kt, vslot, s0, s1) in tinfo:
                        qs = max(w_lo, a - R)
                        qe = min(w_hi, a + n + R)
                        if qs < qe:
                            geo.append((a, n, kt, vslot, s0, s1, qs, qe))
                    assert geo
                    for gi_, (a, n, kt, vslot, s0, s1, qs, qe) in enumerate(geo):
                        for hf in range(2):
                            units.append(dict(hf=hf, rows=slice(64 * hf, 64 * hf + 64), d=d, r=r, scal=8.0 * slopes2[hf] * d,
                                              a=a, n=n, kt=kt, vslot=vslot, s0=s0, s1=s1, qs=qs, qe=qe, w_lo=w_lo, w_hi=w_hi,
                                              ns=2 * (widx % 2) + hf, first=(gi_ == 0), last=(gi_ == len(geo) - 1)))
                    widx += 1
                    w_lo = w_hi

            def u_head(i, u):
                n, nq = u["n"], u["qe"] - u["qs"]
                d, r, rows = u["d"], u["r"], u["rows"]
                c0 = u["qs"] - (u["a"] - R)
                sq0 = r + d * u["qs"] - q_lo
                sq1 = r + d * (u["qe"] - 1) - q_lo + 1
                bs = state["sslot"]; state["sslot"] = (bs + 1) % 4
                ps_ = state["sbslot"]; state["sbslot"] = (ps_ + 1) % NSB
                u["ps"] = ps_
                pb = bankS[bs]
                mm(pb[0:n, 0:nq], KT[rows, u["s0"]:u["s1"]:d], QT[rows, sq0:sq1:d], True, True,
                   blk_keys("KT", u["s0"], u["s1"]) + blk_keys("QT", sq0, sq1), [f"S{bs}"])
                dve(lambda e, pb=pb, n=n, nq=nq, c0=c0, scal=u["scal"]:
                    e.scalar_tensor_tensor(out=pb[0:n, 0:nq], in0=Dm[0:n, c0:c0 + nq], scalar=scal,
                                           in1=pb[0:n, 0:nq], op0=ALU.mult, op1=ALU.add),
                    [Dm_key, f"S{bs}"], [f"S{bs}"])
                act(PT[ps_][0:n, 0:nq], pb[0:n, 0:nq], AF.Exp, [f"S{bs}", "vb"], [f"PT{ps_}"],
                    scale=0.125, bias=vb[0:n, u["kt"]:u["kt"] + 1])

            def u_tail(i, u):
                n, nq = u["n"], u["qe"] - u["qs"]
                d, r, ns, hf = u["d"], u["r"], u["ns"], u["hf"]
                ps_ = u["ps"]
                o0 = u["qs"] - u["w_lo"]
                wb = WB[ns]
                wkey = WBK[ns]
                mm(wb[:, o0:o0 + nq], Vx[0:n, u["vslot"], 64 * hf:64 * hf + 128], PT[ps_][0:n, 0:nq], u["first"], u["last"],
                   [f"Vx{u['vslot']}", f"PT{ps_}"], [wkey], sgc=True)
                if not u["last"]:
                    return
                w_lo, w_hi = u["w_lo"], u["w_hi"]
                nw = w_hi - w_lo
                c_lo = r + d * w_lo - a0
                c_hi = r + d * (w_hi - 1) - a0 + 1
                acc = accn if hf == 0 else accd
                akeys = [f"acc{hf}_{b_}" for b_ in range(c_lo // 512, (c_hi - 1) // 512 + 1)]
                if d == first_cfg_d:
                    if sink_cols is None:
                        act(acc[:, c_lo:c_hi:d], wb[:, 0:nw], AF.Copy, [wkey], akeys)
                    else:
                        sc_ = sink_cols[hf]
                        act(acc[:, c_lo:c_hi:d], wb[:, 0:nw], AF.Identity, [wkey, "sinkc"], akeys, bias=sinkc[:, sc_:sc_ + 1])
                else:
                    dve(lambda e, acc=acc, c_lo=c_lo, c_hi=c_hi, d=d, wb=wb, nw=nw:
                        e.tensor_tensor(out=acc[:, c_lo:c_hi:d], in0=wb[:, 0:nw], in1=acc[:, c_lo:c_hi:d], op=ALU.add),
                        [wkey] + akeys, akeys)

            for i in range(0, len(units) + LA, 2):
                for j in (i, i + 1):
                    if j < len(units):
                        u_head(j, units[j])
                for j in (i - LA, i + 1 - LA):
                    if 0 <= j < len(units):
                        u_tail(j, units[j])

        def normalize_pair(chunk):
            c = q_lo - a0
            c_end = q_hi - a0
            while c < c_end:
                n = min(512, c_end - c)
                t_ = state["qslot"]; state["qslot"] ^= 1
                k0 = [f"acc0_{b_}" for b_ in range(c // 512, (c + n - 1) // 512 + 1)]
                k1 = [f"acc1_{b_}" for b_ in range(c // 512, (c + n - 1) // 512 + 1)]
                mm(bankN[t_][:, 0:n], sel[:, 0:128], accn[:, c:c + n], True, False, ["sel"] + k0, [f"N{t_}"])
                mm(bankN[t_][:, 0:n], sel[:, 128:256], accd[:, c:c + n], False, True, ["sel"] + k1, [f"N{t_}"])
                dve(lambda e, c=c, n=n, t_=t_: e.tensor_scalar(out=tmpf[t_][:, 0:n], in0=bankN[t_][:, 0:n], scalar1=1e-30, scalar2=None, op0=ALU.max),
                    [f"N{t_}"], [f"tmpf{t_}"])
                act(tmpf[t_][:, 0:n], tmpf[t_][:, 0:n], AF.Ln, [f"tmpf{t_}"], [f"tmpf{t_}"])
                act(rsf[t_][:, 0:n], tmpf[t_][:, 0:n], AF.Exp, [f"tmpf{t_}"], [f"rsf{t_}"], scale=-1.0)
                dve(lambda e, c=c, n=n, t_=t_, chunk=chunk: e.tensor_tensor(out=yT[0:64, chunk, c:c + n], in0=accn[0:64, c:c + n], in1=rsf[t_][0:64, 0:n], op=ALU.mult),
                    k0 + [f"rsf{t_}"], [f"yT{chunk}"])
                dve(lambda e, c=c, n=n, t_=t_, chunk=chunk: e.tensor_tensor(out=yT[64:128, chunk, c:c + n], in0=accd[64:128, c:c + n], in1=rsf[t_][64:128, 0:n], op=ALU.mult),
                    k1 + [f"rsf{t_}"], [f"yT{chunk}"])
                c += n

        def out_norm(m):
            c = 0
            ncol = o_hi - o_lo + 2
            while c < ncol:
                n = min(256, ncol - c)
                t_ = state["qslot"]; state["qslot"] ^= 1
                sq = [sqb[j // 2][:, (j % 2) * 256:(j % 2) * 256 + n] for j in range(4)]
                for j in range(4):
                    ch = 4 * m + j
                    act(sq[j], yT[:, ch, c:c + n], AF.Square, [f"yT{ch}"], [f"sqb{j // 2}"])
                for j in range(4):
                    mm(bankN[t_][:, 0:n], onesb[:], sq[j], j == 0, j == 3, ["onesb", f"sqb{j // 2}"], [f"N{t_}"])
                rstd_chain(bankN[t_][:, 0:n], rsf[t_][:, 0:n], tmpf[t_][:, 0:n], 1.0 / 512, f"N{t_}", f"tmpf{t_}", f"rsf{t_}")
                for j in range(4):
                    ch = 4 * m + j
                    dve(lambda e, ch=ch, c=c, n=n, t_=t_: e.scalar_tensor_tensor(out=yT[:, ch, c:c + n], in0=yT[:, ch, c:c + n],
                                                                                scalar=pvec[:, 4 + ch:5 + ch], in1=rsf[t_][:, 0:n],
                                                                                op0=ALU.mult, op1=ALU.mult),
                        [f"yT{ch}", "pvec", f"rsf{t_}"], [f"yT{ch}"])
                c += n

        kt_index = build_program.kt_index
        for p in range(4):
            load_w(0, 0, [(0, 128 * p, 128)])
            load_w(1, 0, [(0, 512 + 128 * p, 128)])
            load_w(2, 0, [(0, 1024 + 128 * p, 128)])
            project_all([(2, VT, "VT", k_lo, k_hi, None, 0), (0, QTv(QT, q_lo), "QT", q_lo, q_hi, 0, q_lo),
                         (1, KT, "KT", k_lo, k_hi, 1, 0)], with_stage1=(p == 0))
            attention(64, (1, 4, 16), DA, "DA", (SLOPES_A[2 * p], SLOPES_A[2 * p + 1]), k_lo, k_hi, None)
            normalize_pair(p)
        kb_lo, kb_hi = max(k_lo, q_lo - 128), min(k_hi, q_hi + 128)
        for p in range(4):
            g = p // 2
            load_w(0, 0, [(0, 1536 + 128 * p, 128)])
            if p % 2 == 0:
                load_w(1, 0, [(0, 2048 + 64 * g, 64), (64, 2048 + 64 * g, 64)])
                load_w(2, 0, [(0, 2176 + 64 * g, 64), (64, 2176 + 64 * g, 64)])
                project_all([(2, VT, "VT", kb_lo, kb_hi, None, 0), (0, QTv(QT, q_lo), "QT", q_lo, q_hi, 2, q_lo),
                             (1, KT, "KT", kb_lo, kb_hi, 3, 0)])
            else:
                project_all([(0, QTv(QT, q_lo), "QT", q_lo, q_hi, 2, q_lo)])
            def prefetch_ffn_w():
                dead = [f"hT{t}" for t in range(NTOK_MAX // 128)] + ["wsl0", "wsl1", "wsl2"] + [f"VT{b}" for b in range(cdiv(NTOK_MAX, 512))]
                for j in range(2):
                    dma("sp", FB["wd"][:, 11 * j:11 * (j + 1), :].rearrange("p c n -> p (c n)"),
                        SCR["wd_s"][:, 11 * j * D:11 * (j + 1) * D], ["wd_s"], ["wd"] + dead)
                dma("sp", FB["wo"].rearrange("p c n -> p (c n)"), SCR["wo_s"], ["wo_s"], ["wo"] + dead)
            attention(128, (1,), DB, "DB", (SLOPES_B[2 * p], SLOPES_B[2 * p + 1]), kb_lo, kb_hi, (2 * p, 2 * p + 1),
                      after_vx=(prefetch_ffn_w if (p == 3 and si > 0) else None), reuse_vx=(p % 2 == 1))
            normalize_pair(4 + p)
            if p == 0:
                out_norm(0)
        out_norm(1)

        fence()
        ffn_phase(nc, P, es, sg, si, xin, yT, xt, hb, st_ss, junkb, n2bc, convp, identb, epsc,
                  w_out, w_up, w_down, out_p if si < 2 else out_s, (si * 1024 if si < 2 else (si - 2) * 2048),
                  bankS, bankN, bankD, bankT, state, act, dve, pool, mm, tr, dma, rstd_chain, FB)
        fence()

    return nc, P, es, kt_tab


class QTv:
    def __init__(self, t, q_lo):
        self.t, self.q_lo = t, q_lo

    def __getitem__(self, key):
        rows, cols = key
        return self.t[rows, cols.start - self.q_lo:cols.stop - self.q_lo]


build_program.kt_index = {}
QV_TAB = []


def ffn_phase(nc, P, es, sg, si, xin, yT, xt, hb, st_ss, junkb, n2bc, convp, identb, epsc,
              w_out, w_up, w_down, out_ap, out_row0, bankS, bankN, bankD, bankT, state,
              act, dve, pool, mm, tr, dma, rstd_chain, B):
    o_lo, o_hi = sg["o_lo"], sg["o_hi"]
    a0 = o_lo - 1
    wo, wd, x1, h2T, gT = B["wo"], B["wd"], B["x1"], B["h2T"], B["gT"]
    SCR = B["SCR"]
    if si == 0:
        k = 0
        for kc in range(8):
            s_ = k % 2; k += 1
            dma("sp", B["ot"][s_][:], w_out[kc * 128:(kc + 1) * 128, :], [], [f"ot{s_}"])
            dve(lambda e, s_=s_, kc=kc: e.tensor_copy(out=wo[:, kc, :], in_=B["ot"][s_][:]), [f"ot{s_}"], ["wo"])
        dma("pool", SCR["wo_s"], wo.rearrange("p c n -> p (c n)"), ["wo"], ["wo_s"])
        for fc in range(22):
            s_ = k % 2; k += 1
            dma("sp", B["ot"][s_][:], w_down[fc * 128:(fc + 1) * 128, :], [], [f"ot{s_}"])
            dve(lambda e, s_=s_, fc=fc: e.tensor_copy(out=wd[:, fc, :], in_=B["ot"][s_][:]), [f"ot{s_}"], ["wd"])
        dma("pool", SCR["wd_s"], wd.rearrange("p c n -> p (c n)"), ["wd"], ["wd_s"])
    else:
        pass

    yT_keys = [f"yT{c}" for c in range(8)]
    wur = w_up.rearrange("(c p) n -> p c n", p=128)
    PAIRS = [(bankS, "S"), (bankN, "N"), (bankD, "D")]

    blocks = []
    b0 = o_lo
    while b0 < o_hi:
        s_first = b0 - 1
        n = min(FBLK + 2, o_hi + 1 - s_first)
        tiles = []
        c = 0
        while c < n:
            m = min(128, n - c)
            tiles.append((c, m))
            c += m
        blocks.append(dict(b0=b0, s_first=s_first, n=n, tiles=tiles))
        b0 += FBLK

    def s4_tail(bi, ti, hs):
        (c, m) = blocks[bi]["tiles"][ti]
        tsl = state["tslot"]; state["tslot"] ^= 1
        tk = ["S3", "S2"][tsl]
        for kc in range(8):
            tr(bankT[tsl][:, kc * 128:kc * 128 + m], hb[hs][0:m, kc * 128:(kc + 1) * 128], identb[0:m, 0:m],
               [f"hb{hs}", "identb"], [tk])
        src = bankT[tsl].rearrange("p (c t) -> p c t", c=8)[:, :, 0:m]
        act(h2T[:, :, c:c + m], src, AF.Copy, [tk], ["h2T"])

    def s4_tile(bi, ti):
        blk = blocks[bi]
        (c, m) = blk["tiles"][ti]
        s_first, n, tiles = blk["s_first"], blk["n"], blk["tiles"]
        xs = state["xslot"]; state["xslot"] ^= 1
        xtile = xt[xs]
        dma("sp", xtile[0:m, :], xin[s_first + c:s_first + c + m, :], [], [f"xt{xs}"])
        pp = PAIRS[0]
        yc0 = s_first + c - a0
        for half in range(2):
            for kc in range(8):
                mm(pp[0][half][0:m, :], yT[:, kc, yc0:yc0 + m], wo[:, kc, half * 512:(half + 1) * 512], kc == 0, kc == 7,
                   yT_keys + ["wo"], [f"{pp[1]}{half}"])
        for half in range(2):
            dve(lambda e, pp=pp, half=half, m=m, ti=ti, xtile=xtile:
                e.tensor_tensor(out=x1[0:m, ti, half * 512:(half + 1) * 512], in0=pp[0][half][0:m, :],
                                in1=xtile[0:m, half * 512:(half + 1) * 512], op=ALU.add),
                [f"{pp[1]}{half}", f"xt{xs}"], [f"x1_{ti}"])
        c3 = 3 * (ti % 2)
        hs = ti % 2
        act(junkb[0:m, :], x1[0:m, ti, :], AF.Square, [f"x1_{ti}"], [f"ss{c3}"], accum=st_ss[0:m, c3:c3 + 1])
        rstd_chain(st_ss[0:m, c3:c3 + 1], st_ss[0:m, c3 + 2:c3 + 3], st_ss[0:m, c3 + 1:c3 + 2], 1.0 / D, f"ss{c3}", f"ss{c3 + 1}", f"ss{c3 + 2}")
        specials = []
        if ti == 0 and blk["b0"] == o_lo:
            specials.append((0, o_lo - 1))
        if ti == len(tiles) - 1 and s_first + n == o_hi + 1:
            specials.append((m - 1, o_hi))
        for (row, stok) in specials:
            col = len(QV_TAB)
            QV_TAB.append((si, row, stok))
            dve(lambda e, m=m, col=col, c3=c3: e.tensor_tensor(out=st_ss[0:m, c3 + 2:c3 + 3], in0=st_ss[0:m, c3 + 2:c3 + 3], in1=B["qv"][0:m, col:col + 1], op=ALU.mult),
                [f"ss{c3 + 2}", "qv"], [f"ss{c3 + 2}"])
        dve(lambda e, m=m, ti=ti, hs=hs, c3=c3: e.scalar_tensor_tensor(out=hb[hs][0:m, :], in0=x1[0:m, ti, :], scalar=st_ss[0:m, c3 + 2:c3 + 3], in1=n2bc[0:m, :],
                                                          op0=ALU.mult, op1=ALU.mult),
            [f"x1_{ti}", f"ss{c3 + 2}", "n2bc"], [f"hb{hs}"])
        return (bi, ti, hs)

    def u_phase(bi):
        n = blocks[bi]["n"]
        gkeys = [f"gT{f_}" for f_ in range(22)]
        pool(lambda e: e.memset(gT[:, :, 0:1], 0.0), [], gkeys)
        pool(lambda e, n=n: e.memset(gT[:, :, n - 1:n], 0.0), [], gkeys)
        for fc in range(22):
            ws = fc % 3
            if si == 0 and bi == 0:
                wf = B["ot"]
                for gi in range(2):
                    dma("sp", wf[gi].rearrange("p (c n) -> p c n", c=8), wur[:, :, gi * DFF + fc * 128:gi * DFF + (fc + 1) * 128],
                        [], [f"ot{gi}"])
                    (dve if gi == 0 else pool)(lambda e, ws=ws, gi=gi, wf=wf: e.tensor_copy(out=B["wu"][ws][:, :, gi * 128:(gi + 1) * 128],
                                                                   in_=wf[gi].rearrange("p (c n) -> p c n", c=8)),
                        [f"ot{gi}"], [f"wu{ws}"])
                dma("pool", SCR["wu_s"][fc], B["wu"][ws].rearrange("p c n -> p (c n)"), [f"wu{ws}"], [f"wu_s{fc}"])
            else:
                dma("sp", B["wu"][ws].rearrange("p c n -> p (c n)"), SCR["wu_s"][fc], [f"wu_s{fc}"], [f"wu{ws}"])
            pp = PAIRS[fc % 3]
            cs = fc % 3
            for gi in range(2):
                pb = pp[0][gi]
                pk = f"{pp[1]}{gi}"
                for kc in range(8):
                    mm(pb[:, 0:n], B["wu"][ws][:, kc, gi * 128:(gi + 1) * 128], h2T[:, kc, 0:n], kc == 0, kc == 7,
                       [f"wu{ws}", "h2T"], [pk])
                cbuf = (B["cg"] if gi == 0 else B["cv"])[cs]
                ckey = ("cg" if gi == 0 else "cv") + str(cs)
                fidx = fc + 22 * gi
                act(cbuf[:, 1:n - 1], pb[:, 1:n - 1], AF.Identity, [pk, "convp"], [ckey],
                    scale=convp[:, 44 + fidx:45 + fidx], bias=convp[:, 132 + fidx:133 + fidx])
                dve(lambda e, cbuf=cbuf, pb=pb, n=n, fidx=fidx: e.scalar_tensor_tensor(
                    out=cbuf[:, 1:n - 1], in0=pb[:, 0:n - 2], scalar=convp[:, fidx:fidx + 1], in1=cbuf[:, 1:n - 1],
                    op0=ALU.mult, op1=ALU.add), [pk, "convp", ckey], [ckey])
                dve(lambda e, cbuf=cbuf, pb=pb, n=n, fidx=fidx: e.scalar_tensor_tensor(
                    out=cbuf[:, 1:n - 1], in0=pb[:, 2:n], scalar=convp[:, 88 + fidx:89 + fidx], in1=cbuf[:, 1:n - 1],
                    op0=ALU.mult, op1=ALU.add), [pk, "convp", ckey], [ckey])
            act(B["sgt"][cs][:, 1:n - 1], B["cg"][cs][:, 1:n - 1], AF.Silu, [f"cg{cs}"], [f"sgt{cs}"])
            pool(lambda e, cs=cs, fc=fc, n=n: e.tensor_tensor(out=gT[:, fc, 1:n - 1], in0=B["sgt"][cs][:, 1:n - 1],
                                                            in1=B["cv"][cs][:, 1:n - 1], op=ALU.mult),
                 [f"sgt{cs}", f"cv{cs}"], [f"gT{fc}"])

    def d_tile(bi, ti):
        blk = blocks[bi]
        (c, m) = blk["tiles"][ti]
        s_first, tiles = blk["s_first"], blk["tiles"]
        pp = PAIRS[1 + ti % 2]
        for half in range(2):
            for fc in range(22):
                mm(pp[0][half][0:m, :], gT[:, fc, c:c + m], wd[:, fc, half * 512:(half + 1) * 512], fc == 0, fc == 21,
                   [f"gT{fc}", "wd"], [f"{pp[1]}{half}"])
        os_ = ti % 2
        for half in range(2):
            dve(lambda e, pp=pp, half=half, m=m, ti=ti, os_=os_:
                e.tensor_tensor(out=B["ot"][os_][0:m, half * 512:(half + 1) * 512], in0=pp[0][half][0:m, :],
                                in1=x1[0:m, ti, half * 512:(half + 1) * 512], op=ALU.add),
                [f"{pp[1]}{half}", f"x1_{ti}"], [f"ot{os_}"])
        r0 = 1 if ti == 0 else 0
        r1 = m - 1 if ti == len(tiles) - 1 else m
        if r1 > r0:
            tok0 = s_first + c + r0 - o_lo
            dma("pool", out_ap[out_row0 + tok0:out_row0 + tok0 + (r1 - r0), :], B["ot"][os_][r0:r1, :], [f"ot{os_}"], [])

    pending = None
    for ti in range(len(blocks[0]["tiles"])):
        nxt = s4_tile(0, ti)
        if pending is not None:
            s4_tail(*pending)
        pending = nxt
    s4_tail(*pending)
    for bi in range(len(blocks)):
        u_phase(bi)
        nt = len(blocks[bi]["tiles"])
        nt_next = len(blocks[bi + 1]["tiles"]) if bi + 1 < len(blocks) else 0
        pending = None
        for ti in range(max(nt, nt_next)):
            if ti < nt:
                d_tile(bi, ti)
            if ti < nt_next:
                nxt = s4_tile(bi + 1, ti)
                if pending is not None:
                    s4_tail(*pending)
                pending = nxt
        if pending is not None:
            s4_tail(*pending)


def build_all():
    nc, P, es, kt_tab = build_program()
    P.schedule()
    with ExitStack() as es2:
        sems = {e: es2.enter_context(nc.semaphore(f"sem_{e}")) for e in Prog.ENGS}
        dsems = {}
        for q in ("sp", "pool"):
            for i in range(P.NDS):
                dsems[(q, i)] = es2.enter_context(nc.semaphore(f"dsem_{q}_{i}"))
        block = es2.enter_context(nc.Block())
        run = emit_program(nc, P, sems, dsems)

        @block.sync
        def _(e):
            run("sp", e)

        @block.tensor
        def _(e):
            run("pe", e)

        @block.scalar
        def _(e):
            run("act", e)

        @block.vector
        def _(e):
            run("dve", e)

        @block.gpsimd
        def _(e):
            run("pool", e)
    es.close()
    return nc, kt_tab


_cache = {}


def kernel(x_prompt, x_sample, norm1, w_in, q_norm_a, k_norm_a, q_norm_b, k_norm_b, sink_b,
           out_norm_a, out_norm_b, w_out, norm2, w_up, conv_w, conv_b, w_down):
    if "nc" not in _cache:
        _cache["nc"] = build_all()
    nc, kt_tab = _cache["nc"]
    f = np.float32
    x_prompt = np.asarray(x_prompt, f)
    x_sample = np.asarray(x_sample, f)
    ident = np.eye(128, dtype=f)
    blk = np.zeros((128, 128), f)
    blk[:64, :64] = 1
    blk[64:, 64:] = 1
    i = np.arange(128)[:, None]
    c = np.arange(256)[None, :]
    dist = np.abs(i - c + 64)
    da = np.where(dist <= 64, -dist, -1e5).astype(f)
    c = np.arange(384)[None, :]
    dist = np.abs(i - c + 128)
    db = np.where(dist <= 128, -dist, -1e5).astype(f)
    pvec = np.zeros((128, 16), f)
    pvec[:, 0] = np.tile(np.asarray(q_norm_a, f)[0], 2)
    pvec[:, 1] = np.tile(np.asarray(k_norm_a, f)[0], 2)
    pvec[:, 2] = np.tile(np.asarray(q_norm_b, f)[0], 2)
    pvec[:, 3] = np.tile(np.asarray(k_norm_b, f)[0], 2)
    pvec[:, 4:8] = np.asarray(out_norm_a, f)[0].reshape(4, 128).T
    pvec[:, 8:12] = np.asarray(out_norm_b, f)[0].reshape(4, 128).T
    cw = np.asarray(conv_w, f)[0]
    cb = np.asarray(conv_b, f)[0]
    convp = np.concatenate([cw[0].reshape(44, 128).T, cw[1].reshape(44, 128).T, cw[2].reshape(44, 128).T,
                            cb.reshape(44, 128).T], axis=1).astype(f)
    common = dict(w_in=np.ascontiguousarray(np.asarray(w_in, f)[0]), w_out=np.ascontiguousarray(np.asarray(w_out, f)[0]),
                  w_up=np.ascontiguousarray(np.asarray(w_up, f)[0]), w_down=np.ascontiguousarray(np.asarray(w_down, f)[0]),
                  c_ident=ident, c_blk=blk, c_da=da, c_db=db, c_pvec=pvec, c_conv=np.ascontiguousarray(convp),
                  c_n1=np.asarray(norm1, f).reshape(1, D), c_n2=np.asarray(norm2, f).reshape(1, D),
                  c_sink=np.asarray(sink_b, f).reshape(1, 8))
    selm = np.zeros((128, 256), f)
    for j in range(64):
        selm[64 + j, j] = 1.0
        selm[j, 128 + 64 + j] = 1.0
    m2m = np.zeros((128, 8), f)
    for h in range(8):
        if h % 2 == 0:
            m2m[64:, h] = 1.0
        else:
            m2m[:64, h] = 1.0
    common.update(c_sel=selm, c_m2=m2m)
    in_maps = []
    for core in range(NCORES):
        xps, tlos = [], []
        for h in range(2):
            x0 = np.zeros((3328, D), f)
            t_lo = 2048 * core + 1024 * h - 1152
            lo, hi = max(0, t_lo), min(16384, t_lo + 3328)
            x0[lo - t_lo:hi - t_lo] = x_prompt[0, lo:hi]
            xps.append(x0)
            tlos.append(t_lo)
        xs = []
        for j in range(2):
            xx = np.zeros((2304, D), f)
            xx[128:2176] = x_sample[2 * core + j]
            xs.append(xx)
        vbt = np.zeros((128, NKT), f)
        assert len(kt_tab) <= NKT, len(kt_tab)
        for col, (si, pos) in enumerate(kt_tab):
            if si < 2:
                t = tlos[si] + np.asarray(pos)
                vbt[:len(pos), col] = np.where((t >= 0) & (t < 16384), 0.0, NEG)
        qvt = np.ones((128, 16), f)
        assert len(QV_TAB) <= 16
        for col, (si, row, stok) in enumerate(QV_TAB):
            if si < 2:
                t = tlos[si] + stok
                qvt[row, col] = 1.0 if (0 <= t < 16384) else 0.0
            else:
                qvt[row, col] = 0.0
        m = dict(common)
        m.update(x0=xps[0], x1=xps[1], x2=xs[0], x3=xs[1], c_vb=vbt, c_qv=qvt)
        in_maps.append(m)
    res = run_bass_kernel_spmd(nc, in_maps, core_ids=list(range(NCORES)))
    y_p = np.concatenate([r["out_p"] for r in res.results], axis=0).reshape(1, 16384, D)
    y_s = np.concatenate([r["out_s"].reshape(2, 2048, D) for r in res.results], axis=0)
    return (y_p.astype(f), y_s.astype(f))
```

```python
import numpy as np
from contextlib import ExitStack
import concourse.bass as bass
import concourse.mybir as mybir
from concourse.bass_utils import run_bass_kernel_spmd

F32 = mybir.dt.float32
BF16 = mybir.dt.bfloat16
ALU = mybir.AluOpType
AF = mybir.ActivationFunctionType

NCORES = 8
D = 1024
DFF = 2816
EPS = 1e-6
NEG = -30000.0
SAME_ENGINE_SYNC = True

_s = np.exp2(-8.0 * np.arange(1, 17, dtype=np.float32) / 16).astype(np.float32)
SLOPES_A = [float(v) for v in _s[8:]]
SLOPES_B = [float(v) for v in _s[:8]]

SEGS = [
    dict(name="p0", nrows=3328, t0=0, t1=26, k_lo=127, k_hi=3201, q_lo=1151, q_hi=2177, o_lo=1152, o_hi=2176),
    dict(name="p1", nrows=3328, t0=0, t1=26, k_lo=127, k_hi=3201, q_lo=1151, q_hi=2177, o_lo=1152, o_hi=2176),
    dict(name="a", nrows=2304, t0=1, t1=17, k_lo=128, k_hi=2176, q_lo=128, q_hi=2176, o_lo=128, o_hi=2176),
    dict(name="b", nrows=2304, t0=1, t1=17, k_lo=128, k_hi=2176, q_lo=128, q_hi=2176, o_lo=128, o_hi=2176),
]
NTOK_MAX = 3328
NACC = 2050
NKT = 320
FBLK = 342


def cdiv(a, b):
    return -(-a // b)


class Prog:
    ENGS = ("pe", "act", "dve", "pool", "sp")

    def __init__(self):
        self.ops = []

    def op(self, eng, fn, reads=(), writes=(), dma=False, fence=False):
        reads = tuple(reads) + (() if fence else ("PHASE",))
        writes = tuple(writes) + (("PHASE",) if fence else ())
        self.ops.append(dict(eng=eng, fn=fn, reads=reads, writes=writes, dma=dma, fence=fence))

    def schedule(self):
        ops = self.ops
        last_w = {}
        readers = {}
        for i, o in enumerate(ops):
            deps = set()
            raw = set()
            for k in o["reads"]:
                if k in last_w:
                    deps.add(last_w[k])
                    raw.add(last_w[k])
            o["raw"] = raw
            for k in o["writes"]:
                if k in last_w:
                    deps.add(last_w[k])
                for r in readers.get(k, ()):
                    deps.add(r)
            deps.discard(i)
            if o["fence"]:
                deps = set(d for d in deps if not (ops[d]["dma"] and ops[d].get("consumed")))
            for d in deps:
                if ops[d]["dma"]:
                    ops[d]["consumed"] = True
            o["deps"] = deps
            for k in o["writes"]:
                last_w[k] = i
                readers[k] = []
            for k in o["reads"]:
                readers.setdefault(k, []).append(i)
        for i, o in enumerate(ops):
            best = {}
            keep = []
            for d in o["deps"]:
                od = ops[d]
                if od["dma"]:
                    keep.append(d)
                    continue
                e = od["eng"]
                if e == o["eng"] and not o["dma"]:
                    if e == "pe" or not SAME_ENGINE_SYNC:
                        continue
                    if d not in o["raw"]:
                        continue
                if e not in best or d > best[e]:
                    best[e] = d
            o["deps"] = keep + list(best.values())
        signalled = set()
        for o in ops:
            for d in o["deps"]:
                signalled.add(d)
        self.signalled = signalled
        cnt = {e: 0 for e in self.ENGS}
        for i, o in enumerate(ops):
            if o["dma"]:
                continue
            if i in signalled:
                cnt[o["eng"]] += 1
                o["rank"] = cnt[o["eng"]]
        self.NDS = 24
        dcnt = {}
        slot_hist = {}
        for i, o in enumerate(ops):
            if not o["dma"]:
                continue
            q = o["eng"]
            n = dcnt.get(q, 0)
            dcnt[q] = n + 1
            slot = n % self.NDS
            gen = n // self.NDS + 1
            o["dsem"] = (q, slot)
            o["dval"] = 16 * gen
            o["prev_same_slot"] = slot_hist.get((q, slot))
            slot_hist[(q, slot)] = i
        self.final_dma = slot_hist


def emit_program(nc, prog, sems, dsems):
    ops = prog.ops
    by_eng = {e: [] for e in Prog.ENGS}
    for i, o in enumerate(ops):
        by_eng[o["eng"]].append(i)

    def run_engine(ename, eh):
        waited = {}

        def wait(sem_key, sem, val):
            if waited.get(sem_key, 0) >= val:
                return
            waited[sem_key] = val
            eh.wait_ge(sem, val)

        for i in by_eng[ename]:
            o = ops[i]
            if o["dma"] and o["prev_same_slot"] is not None:
                p = ops[o["prev_same_slot"]]
                wait(("d",) + p["dsem"], dsems[p["dsem"]], p["dval"])
            for d in o["deps"]:
                od = ops[d]
                if od["dma"]:
                    wait(("d",) + od["dsem"], dsems[od["dsem"]], od["dval"])
                else:
                    wait(od["eng"], sems[od["eng"]], od["rank"])
            ins = o["fn"](eh)
            if o["dma"]:
                ins.then_inc(dsems[o["dsem"]], 16)
            elif i in prog.signalled:
                ins.then_inc(sems[ename], 1)
        if ename == "sp":
            for key, i in prog.final_dma.items():
                o = ops[i]
                wait(("d",) + o["dsem"], dsems[o["dsem"]], o["dval"])

    return run_engine


def build_program():
    nc = bass.Bass("TRN2", target_bir_lowering=False)
    P = Prog()
    kt_tab = []

    def din(name, shape):
        return nc.dram_tensor(name, list(shape), F32, kind="ExternalInput").ap()

    x_in = [din("x0", (3328, D)), din("x1", (3328, D)), din("x2", (2304, D)), din("x3", (2304, D))]
    w_in = din("w_in", (D, 2304))
    w_out = din("w_out", (D, D))
    w_up = din("w_up", (D, 2 * DFF))
    w_down = din("w_down", (DFF, D))
    c_ident = din("c_ident", (128, 128))
    c_blk = din("c_blk", (128, 128))
    c_da = din("c_da", (128, 256))
    c_db = din("c_db", (128, 384))
    c_pvec = din("c_pvec", (128, 16))
    c_conv = din("c_conv", (128, 4 * 44))
    c_n1 = din("c_n1", (1, D))
    c_n2 = din("c_n2", (1, D))
    c_sink = din("c_sink", (1, 8))
    c_vb = din("c_vb", (128, NKT))
    c_qv = din("c_qv", (128, 16))
    c_sel = din("c_sel", (128, 256))
    c_m2 = din("c_m2", (128, 8))
    out_p = nc.dram_tensor("out_p", [2048, D], F32, kind="ExternalOutput").ap()
    out_s = nc.dram_tensor("out_s", [4096, D], F32, kind="ExternalOutput").ap()

    wi_s = nc.dram_tensor("wi_s", [18, 128, 1024], BF16).ap()
    wo_s = nc.dram_tensor("wo_s", [128, 8 * D], BF16).ap()
    wd_s = nc.dram_tensor("wd_s", [128, 22 * D], BF16).ap()
    wu_s = nc.dram_tensor("wu_s", [22, 128, 2048], BF16).ap()
    SCR = dict(wi_s=wi_s, wo_s=wo_s, wd_s=wd_s, wu_s=wu_s)

    es = ExitStack()

    def sb(name, shape, dt):
        return es.enter_context(nc.sbuf_tensor(name, list(shape), dt))

    def ps(name, shape, dt):
        return es.enter_context(nc.psum_tensor(name, list(shape), dt))

    cst2 = sb("cst2", (128, 256), F32)
    identf = cst2[:, 0:128]
    identb = sb("identb", (128, 128), BF16)
    blkf = cst2[:, 128:256]
    blkb = sb("blkb", (128, 128), BF16)
    onesb = sb("onesb", (128, 128), BF16)
    DA = sb("DA", (128, 256), F32)
    DB = sb("DB", (128, 384), F32)
    pvec = sb("pvec", (128, 16), F32)
    convp = sb("convp", (128, 176), F32)
    n1bc = sb("n1bc", (128, D), F32)
    n2bc = sb("n2bc", (128, D), F32)
    sinkc = sb("sinkc", (128, 8), F32)
    vb = sb("vb", (128, NKT), F32)
    epsc = sb("epsc", (128, 1), F32)
    qv = sb("qv", (128, 16), F32)
    sel = cst2
    m2 = sb("m2", (128, 8), F32)
    yT = sb("yT", (128, 8, NACC), BF16)
    xt = [sb("xt0", (128, D), F32), sb("xt1", (128, D), F32)]
    hb = [sb("hb0", (128, D), BF16), sb("hb1", (128, D), BF16)]
    st_ss = sb("st_ss", (128, 8), F32)
    junkb = sb("junkb", (128, D), BF16)
    NB_A = 8 * NTOK_MAX + 2048 + 2 * NTOK_MAX + 58 * 192 + 8 * 384 + 3 * 1024 + 2 * 512
    NB_F = 8 * D + 22 * D + 8 * 512 + 22 * 512 + 3 * 2048
    NF_A = 2 * NACC + 3 * 1024 + 4 * 512 + 1024
    NF_F = 4 * D + 9 * 512 + 2 * D
    AB = sb("arena_b", (128, max(NB_A, NB_F)), BF16)
    AFa = sb("arena_f", (128, max(NF_A, NF_F)), F32)

    class Carver:
        def __init__(self, t):
            self.t, self.o = t, 0

        def take(self, n, c=None):
            v = self.t[:, self.o:self.o + n]
            self.o += n
            if c is not None:
                v = v.rearrange("p (c t) -> p c t", c=c)
            return v
    cb_, cf_ = Carver(AB), Carver(AFa)
    hT = cb_.take(8 * NTOK_MAX, 8)
    wsl = [cb_.take(1024, 8) for i in range(3)]
    VT = cb_.take(NTOK_MAX)
    QT = cb_.take(2048)
    KT = cb_.take(NTOK_MAX)
    NVX = 58
    Vx = cb_.take(NVX * 192, NVX)
    NSB = 8
    PT = [cb_.take(384) for i in range(NSB)]
    sqb = [cb_.take(512) for i in range(2)]
    accn = cf_.take(NACC)
    accd = cf_.take(NACC)
    wsl_f = [cf_.take(1024, 8) for i in range(3)]
    tmpf = [cf_.take(512) for i in range(2)]
    rsf = [cf_.take(512) for i in range(2)]
    xt_a = cf_.take(1024)
    cb_, cf_ = Carver(AB), Carver(AFa)
    FB = dict(
        wd=cb_.take(22 * D, 22), wo=cb_.take(8 * D, 8), h2T=cb_.take(8 * 512, 8), gT=cb_.take(22 * 512, 22),
        wu=[cb_.take(2048, 8) for i in range(3)],
        x1=cf_.take(4 * D, 4),
        cg=[cf_.take(512) for i in range(3)], cv=[cf_.take(512) for i in range(3)], sgt=[cf_.take(512) for i in range(3)],
        ot=[cf_.take(D) for i in range(2)],
    )
    FB["SCR"] = SCR
    fence_t = sb("fence_t", (128, 2), F32)

    def fence():
        P.op("dve", lambda e: e.memset(fence_t[:], 0.0), [], [], fence=True)

    PS_S = ps("PS_S", (128, 1024), F32)
    PS_N = ps("PS_N", (128, 1024), F32)
    PS_D = ps("PS_D", (128, 1024), F32)
    PS_X = ps("PS_X", (128, 512), F32)
    PS_T = ps("PS_T", (128, 512), F32)
    bankS = [PS_S[:, 0:512], PS_S[:, 512:1024], PS_X[:, 0:512], PS_T[:, 0:512]]
    bankN = [PS_N[:, 0:512], PS_N[:, 512:1024]]
    bankD = [PS_D[:, 0:512], PS_D[:, 512:1024]]
    bankT = [PS_T[:, 0:512].bitcast(BF16), PS_X[:, 0:512].bitcast(BF16)]
    TKEY = ["S3", "S2"]
    FB_T = TKEY

    def dma(q, out, in_, reads, writes):
        P.op(q, lambda e, out=out, in_=in_: e.dma_start(out=out, in_=in_), reads, writes, dma=True)

    def act(out, in_, func, reads, writes, scale=1.0, bias=None, accum=None):
        def fn(e, out=out, in_=in_, func=func, scale=scale, bias=bias, accum=accum):
            kw = {}
            if bias is not None:
                kw["bias"] = bias
            if accum is not None:
                kw["accum_out"] = accum
            return e.activation(out=out, in_=in_, func=func, scale=scale, **kw)
        P.op("act", fn, reads, writes)

    def dve(fn, reads, writes):
        P.op("dve", fn, reads, writes)

    def pool(fn, reads, writes):
        P.op("pool", fn, reads, writes)

    def mm(out, lhsT, rhs, start, stop, reads, writes, sgc=False):
        def fn(e, out=out, lhsT=lhsT, rhs=rhs, start=start, stop=stop, sgc=sgc):
            return e.matmul(out, lhsT=lhsT, rhs=rhs, start=start, stop=stop, skip_group_check=sgc)
        P.op("pe", fn, reads, writes)

    def tr(out, in_, ident, reads, writes):
        P.op("pe", lambda e, out=out, in_=in_, ident=ident: e.transpose(out, in_, ident), reads, writes)

    def rstd_chain(ss_ap, out_ap, tmp_ap, scale, reads_key, tmp_key, out_key):
        act(tmp_ap, ss_ap, AF.Ln, [reads_key, "epsc"], [tmp_key], scale=scale, bias=epsc[0:ss_ap.shape[0], 0:1])
        act(out_ap, tmp_ap, AF.Exp, [tmp_key], [out_key], scale=-0.5)

    dma("sp", identf, c_ident, [], ["cst2"])
    dma("sp", blkf, c_blk, [], ["cst2"])
    dma("sp", DA[:], c_da, [], ["DA"])
    dma("sp", DB[:], c_db, [], ["DB"])
    dma("sp", pvec[:], c_pvec, [], ["pvec"])
    dma("sp", convp[:], c_conv, [], ["convp"])
    dma("sp", n1bc[:], c_n1.partition_broadcast(128), [], ["n1bc"])
    dma("sp", n2bc[:], c_n2.partition_broadcast(128), [], ["n2bc"])
    dma("sp", sinkc[:], c_sink.partition_broadcast(128), [], ["sinkc"])
    dma("sp", vb[:], c_vb, [], ["vb"])
    dma("sp", qv[:], c_qv, [], ["qv"])
    FB["qv"] = qv
    dve(lambda e: e.tensor_copy(out=identb[:], in_=identf), ["cst2"], ["identb"])
    dve(lambda e: e.tensor_copy(out=blkb[:], in_=blkf), ["cst2"], ["blkb"])
    dve(lambda e: e.memset(onesb[:], 1.0), [], ["onesb"])
    dve(lambda e: e.memset(epsc[:], EPS), [], ["epsc"])
    act(sinkc[:], sinkc[:], AF.Exp, ["sinkc"], ["sinkc"])
    dma("sp", sel[:], c_sel, [], ["cst2", "sel"])
    dma("sp", m2[:], c_m2, [], ["m2"])
    dve(lambda e: e.tensor_tensor(out=sinkc[:], in0=sinkc[:], in1=m2[:], op=ALU.mult), ["sinkc", "m2"], ["sinkc"])

    state = dict(xslot=0, sbslot=0, vxslot=0, tslot=0, qslot=0, nslot=0, sslot=0)

    for si, sg in enumerate(SEGS):
        xin = x_in[si]
        q_lo, q_hi, k_lo, k_hi, o_lo, o_hi = sg["q_lo"], sg["q_hi"], sg["k_lo"], sg["k_hi"], sg["o_lo"], sg["o_hi"]
        a0 = o_lo - 1

        def stage1_tile(ti):
            xs = ti % 3
            xtile = (xt + [xt_a])[xs]
            par = ti % 2
            c3 = 3 * par
            dma("sp", xtile[:], xin[ti * 128:(ti + 1) * 128, :], [], [f"xt{xs}"])
            act(junkb[:], xtile[:], AF.Square, [f"xt{xs}"], [f"ss{c3}"], accum=st_ss[:, c3:c3 + 1])
            rstd_chain(st_ss[:, c3:c3 + 1], st_ss[:, c3 + 2:c3 + 3], st_ss[:, c3 + 1:c3 + 2], 1.0 / D, f"ss{c3}", f"ss{c3 + 1}", f"ss{c3 + 2}")
            dve(lambda e, xtile=xtile, par=par, c3=c3: e.scalar_tensor_tensor(out=hb[par][:], in0=xtile[:], scalar=st_ss[:, c3 + 2:c3 + 3], in1=n1bc[:],
                                                                            op0=ALU.mult, op1=ALU.mult),
                [f"xt{xs}", f"ss{c3 + 2}", "n1bc"], [f"hb{par}"])
            for kc in range(8):
                tr(bankT[par][:, kc * 128:(kc + 1) * 128], hb[par][:, kc * 128:(kc + 1) * 128], identb[:],
                   [f"hb{par}", "identb"], [TKEY[par]])

        def stage1_copy(ti):
            par = ti % 2
            src = bankT[par].rearrange("p (c t) -> p c t", c=8)
            if ti % 2 == 0:
                act(hT[:, :, ti * 128:(ti + 1) * 128], src, AF.Copy, [TKEY[par]], [f"hT{ti}"])
            else:
                dve(lambda e, src=src, ti=ti: e.tensor_copy(out=hT[:, :, ti * 128:(ti + 1) * 128], in_=src),
                    [TKEY[par]], [f"hT{ti}"])

        dve(lambda e: e.memset(Vx[:, :, 64:128], 1.0), [], [f"Vx{v}" for v in range(NVX)])

        def hT_keys(s0, s1):
            return [f"hT{t}" for t in range(s0 // 128, cdiv(s1, 128))]

        def blk_keys(name, s0, s1):
            return [f"{name}{b}" for b in range(s0 // 512, cdiv(s1, 512))]

        if si >= 2:
            dve(lambda e: e.memset(yT[:, :, 0:1], 0.0), [], [f"yT{c}" for c in range(8)])
            dve(lambda e: e.memset(yT[:, :, NACC - 1:NACC], 0.0), [], [f"yT{c}" for c in range(8)])

        def load_w(slot, col0, ncols_list):
            if si == 0:
                for (dc, sc, n) in ncols_list:
                    dma("sp", wsl_f[slot][:, :, dc:dc + n],
                        w_in.rearrange("(c p) n -> p c n", p=128)[:, :, sc:sc + n], [], [f"wslf{slot}"])
                dve(lambda e, slot=slot: e.tensor_copy(out=wsl[slot][:], in_=wsl_f[slot][:]), [f"wslf{slot}"], [f"wsl{slot}"])
                (dc, sc, n) = ncols_list[0]
                b, off = sc // 128, sc % 128
                dma("pool", wi_s[b].rearrange("p (c n) -> p c n", c=8)[:, :, off:off + n], wsl[slot][:, :, 0:n],
                    [f"wsl{slot}"], [f"wi_s{b}_{off}"])
            else:
                for (dc, sc, n) in ncols_list:
                    b, off = sc // 128, sc % 128
                    dma("sp", wsl[slot][:, :, dc:dc + n], wi_s[b].rearrange("p (c n) -> p c n", c=8)[:, :, off:off + n],
                        [f"wi_s{b}_{off}"], [f"wsl{slot}"])

        def project_all(jobs, with_stage1=False):
            blocks = []
            for (slot, dst, dst_name, s_lo, s_hi, gain_col, korg) in jobs:
                s = s_lo
                while s < s_hi:
                    n = min(512, s_hi - s)
                    blocks.append((slot, dst, dst_name, s, n, gain_col, korg))
                    s += n
            if with_stage1:
                pass
            info = {}
            st1 = dict(next=sg["t0"])

            def stage1_upto(col_end):
                while st1["next"] < sg["t1"] and (st1["next"] - 2) * 128 < col_end:
                    stage1_tile(st1["next"])
                    if st1["next"] > sg["t0"]:
                        stage1_copy(st1["next"] - 1)
                    st1["next"] += 1
                if st1["next"] == sg["t1"] and not st1.get("done"):
                    stage1_copy(sg["t1"] - 1)
                    st1["done"] = True

            def head(i):
                (slot, dst, dst_name, s, n, gain_col, korg) = blocks[i]
                bs = state["sslot"]; state["sslot"] = (bs + 1) % 4
                pb = bankS[bs]
                for kc in range(8):
                    mm(pb[:, 0:n], wsl[slot][:, kc, :], hT[:, kc, s:s + n], kc == 0, kc == 7,
                       [f"wsl{slot}"] + hT_keys(s, s + n), [f"S{bs}"])
                wk = blk_keys(dst_name, s - korg, s + n - korg)
                qs_ = i % 2
                if gain_col is None:
                    act(dst[:, s:s + n], pb[:, 0:n], AF.Copy, [f"S{bs}"], wk)
                else:
                    act(sqb[qs_][:, 0:n], pb[:, 0:n], AF.Square, [f"S{bs}"], [f"sqb{qs_}"])
                info[i] = (bs, qs_, wk)

            def tail(i):
                (slot, dst, dst_name, s, n, gain_col, korg) = blocks[i]
                if gain_col is None:
                    return
                (bs, qs_, wk) = info[i]
                pb = bankS[bs]
                mm(bankN[qs_][:, 0:n], blkb[:], sqb[qs_][:, 0:n], True, True, ["blkb", f"sqb{qs_}"], [f"N{qs_}"])
                rstd_chain(bankN[qs_][:, 0:n], rsf[qs_][:, 0:n], tmpf[qs_][:, 0:n], 1.0 / 64, f"N{qs_}", f"tmpf{qs_}", f"rsf{qs_}")
                dve(lambda e, pb=pb, n=n, s=s, qs_=qs_, gain_col=gain_col, dst=dst:
                    e.scalar_tensor_tensor(out=dst[:, s:s + n], in0=pb[:, 0:n], scalar=pvec[:, gain_col:gain_col + 1],
                                           in1=rsf[qs_][:, 0:n], op0=ALU.mult, op1=ALU.mult),
                    [f"S{bs}", "pvec", f"rsf{qs_}"], wk)

            for i in range(len(blocks) + 1):
                if i < len(blocks):
                    if with_stage1:
                        stage1_upto(10 ** 9)
                    head(i)
                if i >= 1:
                    tail(i - 1)
            if with_stage1:
                stage1_upto(10 ** 9)

        def attention(R, configs, Dm, Dm_key, slopes2, kv_lo, kv_hi, sink_cols, after_vx=None, reuse_vx=False):
            LA = 4
            WB = [bankN[0], bankN[1], bankD[0], bankD[1]]
            WBK = ["N0", "N1", "D0", "D1"]
            classes = []
            vx_jobs = []
            for d in configs:
                kk_lo = max(k_lo, q_lo - R * d, kv_lo)
                kk_hi = min(k_hi, q_hi + R * d, kv_hi)
                for r in range(d):
                    jq_lo, jq_hi = cdiv(q_lo - r, d), cdiv(q_hi - r, d)
                    jk_lo, jk_hi = cdiv(kk_lo - r, d), cdiv(kk_hi - r, d)
                    if jq_hi <= jq_lo or jk_hi <= jk_lo:
                        continue
                    tinfo = []
                    a = jk_lo
                    while a < jk_hi:
                        n = min(128, jk_hi - a)
                        key = (si, d, r, a, n)
                        if key not in kt_index:
                            kt_index[key] = len(kt_tab)
                            kt_tab.append((si, [r + d * (a + i) for i in range(n)]))
                        kt = kt_index[key]
                        s0 = r + d * a
                        s1 = r + d * (a + n - 1) + 1
                        vslot = len(vx_jobs)
                        vx_jobs.append((vslot, s0, s1, d, n))
                        tinfo.append((a, n, kt, vslot, s0, s1))
                        a += n
                    classes.append((d, r, jq_lo, jq_hi, tinfo))
            assert len(vx_jobs) <= NVX, len(vx_jobs)
            if not reuse_vx:
                dve(lambda e: e.memset(bankS[3][:, :], 0.0), [], ["S3"])
                dve(lambda e: e.memset(bankS[2][:, :], 0.0), [], ["S2"])
            for g0 in range(0, 0 if reuse_vx else len(vx_jobs), 8):
                grp = vx_jobs[g0:g0 + 8]
                tb = (g0 // 8) % 2
                tkey = TKEY[tb]
                for j, (vslot, s0, s1, d, n) in enumerate(grp):
                    tr(bankT[tb][0:n, j * 128:(j + 1) * 128], VT[:, s0:s1:d], identb[:], blk_keys("VT", s0, s1) + ["identb"], [tkey])
                ng = len(grp)
                src = bankT[tb][:, 0:ng * 128].rearrange("p (g h c) -> p g h c", g=ng, h=2)
                dstv = Vx[:, g0:g0 + ng, :].rearrange("p g (h c) -> p g h c", h=3)[:, :, 0:3:2, :]
                wkeys = [f"Vx{v}" for (v, _, _, _, _) in grp]
                if (g0 // 8) % 2 == 0:
                    dve(lambda e, src=src, dstv=dstv: e.tensor_copy(out=dstv, in_=src), [tkey], wkeys)
                else:
                    act(dstv, src, AF.Copy, [tkey], wkeys)
            if after_vx is not None:
                after_vx()
            units = []
            first_cfg_d = configs[0]
            widx = 0
            for (d, r, jq_lo, jq_hi, tinfo) in classes:
                w_lo = jq_lo
                while w_lo < jq_hi:
                    w_hi = min(jq_hi, w_lo + 512)
                    geo = []
                    for (a, n, kt, vslot, s0, s1) in tinfo:
                        qs = max(w_lo, a - R)
                        qe = min(w_hi, a + n + R)
                        if qs < qe:
                            geo.append((a, n, kt, vslot, s0, s1, qs, qe))
                    assert geo
                    for gi_, (a, n, kt, vslot, s0, s1, qs, qe) in enumerate(geo):
                        for hf in range(2):
                            units.append(dict(hf=hf, rows=slice(64 * hf, 64 * hf + 64), d=d, r=r, scal=8.0 * slopes2[hf] * d,
                                              a=a, n=n, kt=kt, vslot=vslot, s0=s0, s1=s1, qs=qs, qe=qe, w_lo=w_lo, w_hi=w_hi,
                                              ns=2 * (widx % 2) + hf, first=(gi_ == 0), last=(gi_ == len(geo) - 1)))
                    widx += 1
                    w_lo = w_hi

            def u_head(i, u):
                n, nq = u["n"], u["qe"] - u["qs"]
                d, r, rows = u["d"], u["r"], u["rows"]
                c0 = u["qs"] - (u["a"] - R)
                sq0 = r + d * u["qs"] - q_lo
                sq1 = r + d * (u["qe"] - 1) - q_lo + 1
                bs = state["sslot"]; state["sslot"] = (bs + 1) % 4
                ps_ = state["sbslot"]; state["sbslot"] = (ps_ + 1) % NSB
                u["ps"] = ps_
                pb = bankS[bs]
                mm(pb[0:n, 0:nq], KT[rows, u["s0"]:u["s1"]:d], QT[rows, sq0:sq1:d], True, True,
                   blk_keys("KT", u["s0"], u["s1"]) + blk_keys("QT", sq0, sq1), [f"S{bs}"])
                dve(lambda e, pb=pb, n=n, nq=nq, c0=c0, scal=u["scal"]:
                    e.scalar_tensor_tensor(out=pb[0:n, 0:nq], in0=Dm[0:n, c0:c0 + nq], scalar=scal,
                                           in1=pb[0:n, 0:nq], op0=ALU.mult, op1=ALU.add),
                    [Dm_key, f"S{bs}"], [f"S{bs}"])
                act(PT[ps_][0:n, 0:nq], pb[0:n, 0:nq], AF.Exp, [f"S{bs}", "vb"], [f"PT{ps_}"],
                    scale=0.125, bias=vb[0:n, u["kt"]:u["kt"] + 1])

            def u_tail(i, u):
                n, nq = u["n"], u["qe"] - u["qs"]
                d, r, ns, hf = u["d"], u["r"], u["ns"], u["hf"]
                ps_ = u["ps"]
                o0 = u["qs"] - u["w_lo"]
                wb = WB[ns]
                wkey = WBK[ns]
                mm(wb[:, o0:o0 + nq], Vx[0:n, u["vslot"], 64 * hf:64 * hf + 128], PT[ps_][0:n, 0:nq], u["first"], u["last"],
                   [f"Vx{u['vslot']}", f"PT{ps_}"], [wkey], sgc=True)
                if not u["last"]:
                    return
                w_lo, w_hi = u["w_lo"], u["w_hi"]
                nw = w_hi - w_lo
                c_lo = r + d * w_lo - a0
                c_hi = r + d * (w_hi - 1) - a0 + 1
                acc = accn if hf == 0 else accd
                akeys = [f"acc{hf}_{b_}" for b_ in range(c_lo // 512, (c_hi - 1) // 512 + 1)]
                if d == first_cfg_d:
                    if sink_cols is None:
                        act(acc[:, c_lo:c_hi:d], wb[:, 0:nw], AF.Copy, [wkey], akeys)
                    else:
                        sc_ = sink_cols[hf]
                        act(acc[:, c_lo:c_hi:d], wb[:, 0:nw], AF.Identity, [wkey, "sinkc"], akeys, bias=sinkc[:, sc_:sc_ + 1])
                else:
                    dve(lambda e, acc=acc, c_lo=c_lo, c_hi=c_hi, d=d, wb=wb, nw=nw:
                        e.tensor_tensor(out=acc[:, c_lo:c_hi:d], in0=wb[:, 0:nw], in1=acc[:, c_lo:c_hi:d], op=ALU.add),
                        [wkey] + akeys, akeys)

            for i in range(0, len(units) + LA, 2):
                for j in (i, i + 1):
                    if j < len(units):
                        u_head(j, units[j])
                for j in (i - LA, i + 1 - LA):
                    if 0 <= j < len(units):
                        u_tail(j, units[j])

        def normalize_pair(chunk):
            c = q_lo - a0
            c_end = q_hi - a0
            while c < c_end:
                n = min(512, c_end - c)
                t_ = state["qslot"]; state["qslot"] ^= 1
                k0 = [f"acc0_{b_}" for b_ in range(c // 512, (c + n - 1) // 512 + 1)]
                k1 = [f"acc1_{b_}" for b_ in range(c // 512, (c + n - 1) // 512 + 1)]
                mm(bankN[t_][:, 0:n], sel[:, 0:128], accn[:, c:c + n], True, False, ["sel"] + k0, [f"N{t_}"])
                mm(bankN[t_][:, 0:n], sel[:, 128:256], accd[:, c:c + n], False, True, ["sel"] + k1, [f"N{t_}"])
                dve(lambda e, c=c, n=n, t_=t_: e.tensor_scalar(out=tmpf[t_][:, 0:n], in0=bankN[t_][:, 0:n], scalar1=1e-30, scalar2=None, op0=ALU.max),
                    [f"N{t_}"], [f"tmpf{t_}"])
                act(tmpf[t_][:, 0:n], tmpf[t_][:, 0:n], AF.Ln, [f"tmpf{t_}"], [f"tmpf{t_}"])
                act(rsf[t_][:, 0:n], tmpf[t_][:, 0:n], AF.Exp, [f"tmpf{t_}"], [f"rsf{t_}"], scale=-1.0)
                dve(lambda e, c=c, n=n, t_=t_, chunk=chunk: e.tensor_tensor(out=yT[0:64, chunk, c:c + n], in0=accn[0:64, c:c + n], in1=rsf[t_][0:64, 0:n], op=ALU.mult),
                    k0 + [f"rsf{t_}"], [f"yT{chunk}"])
                dve(lambda e, c=c, n=n, t_=t_, chunk=chunk: e.tensor_tensor(out=yT[64:128, chunk, c:c + n], in0=accd[64:128, c:c + n], in1=rsf[t_][64:128, 0:n], op=ALU.mult),
                    k1 + [f"rsf{t_}"], [f"yT{chunk}"])
                c += n

        def out_norm(m):
            c = 0
            ncol = o_hi - o_lo + 2
            while c < ncol:
                n = min(256, ncol - c)
                t_ = state["qslot"]; state["qslot"] ^= 1
                sq = [sqb[j // 2][:, (j % 2) * 256:(j % 2) * 256 + n] for j in range(4)]
                for j in range(4):
                    ch = 4 * m + j
                    act(sq[j], yT[:, ch, c:c + n], AF.Square, [f"yT{ch}"], [f"sqb{j // 2}"])
                for j in range(4):
                    mm(bankN[t_][:, 0:n], onesb[:], sq[j], j == 0, j == 3, ["onesb", f"sqb{j // 2}"], [f"N{t_}"])
                rstd_chain(bankN[t_][:, 0:n], rsf[t_][:, 0:n], tmpf[t_][:, 0:n], 1.0 / 512, f"N{t_}", f"tmpf{t_}", f"rsf{t_}")
                for j in range(4):
                    ch = 4 * m + j
                    dve(lambda e, ch=ch, c=c, n=n, t_=t_: e.scalar_tensor_tensor(out=yT[:, ch, c:c + n], in0=yT[:, ch, c:c + n],
                                                                                scalar=pvec[:, 4 + ch:5 + ch], in1=rsf[t_][:, 0:n],
                                                                                op0=ALU.mult, op1=ALU.mult),
                        [f"yT{ch}", "pvec", f"rsf{t_}"], [f"yT{ch}"])
                c += n

        kt_index = build_program.kt_index
        for p in range(4):
            load_w(0, 0, [(0, 128 * p, 128)])
            load_w(1, 0, [(0, 512 + 128 * p, 128)])
            load_w(2, 0, [(0, 1024 + 128 * p, 128)])
            project_all([(2, VT, "VT", k_lo, k_hi, None, 0), (0, QTv(QT, q_lo), "QT", q_lo, q_hi, 0, q_lo),
                         (1, KT, "KT", k_lo, k_hi, 1, 0)], with_stage1=(p == 0))
            attention(64, (1, 4, 16), DA, "DA", (SLOPES_A[2 * p], SLOPES_A[2 * p + 1]), k_lo, k_hi, None)
            normalize_pair(p)
        kb_lo, kb_hi = max(k_lo, q_lo - 128), min(k_hi, q_hi + 128)
        for p in range(4):
            g = p // 2
            load_w(0, 0, [(0, 1536 + 128 * p, 128)])
            if p % 2 == 0:
                load_w(1, 0, [(0, 2048 + 64 * g, 64), (64, 2048 + 64 * g, 64)])
                load_w(2, 0, [(0, 2176 + 64 * g, 64), (64, 2176 + 64 * g, 64)])
                project_all([(2, VT, "VT", kb_lo, kb_hi, None, 0), (0, QTv(QT, q_lo), "QT", q_lo, q_hi, 2, q_lo),
                             (1, KT, "KT", kb_lo, kb_hi, 3, 0)])
            else:
                project_all([(0, QTv(QT, q_lo), "QT", q_lo, q_hi, 2, q_lo)])
            def prefetch_ffn_w():
                dead = [f"hT{t}" for t in range(NTOK_MAX // 128)] + ["wsl0", "wsl1", "wsl2"] + [f"VT{b}" for b in range(cdiv(NTOK_MAX, 512))]
                for j in range(2):
                    dma("sp", FB["wd"][:, 11 * j:11 * (j + 1), :].rearrange("p c n -> p (c n)"),
                        SCR["wd_s"][:, 11 * j * D:11 * (j + 1) * D], ["wd_s"], ["wd"] + dead)
                dma("sp", FB["wo"].rearrange("p c n -> p (c n)"), SCR["wo_s"], ["wo_s"], ["wo"] + dead)
            attention(128, (1,), DB, "DB", (SLOPES_B[2 * p], SLOPES_B[2 * p + 1]), kb_lo, kb_hi, (2 * p, 2 * p + 1),
                      after_vx=(prefetch_ffn_w if (p == 3 and si > 0) else None), reuse_vx=(p % 2 == 1))
            normalize_pair(4 + p)
            if p == 0:
                out_norm(0)
        out_norm(1)

        fence()
        ffn_phase(nc, P, es, sg, si, xin, yT, xt, hb, st_ss, junkb, n2bc, convp, identb, epsc,
                  w_out, w_up, w_down, out_p if si < 2 else out_s, (si * 1024 if si < 2 else (si - 2) * 2048),
                  bankS, bankN, bankD, bankT, state, act, dve, pool, mm, tr, dma, rstd_chain, FB)
        fence()

    return nc, P, es, kt_tab


class QTv:
    def __init__(self, t, q_lo):
        self.t, self.q_lo = t, q_lo

    def __getitem__(self, key):
        rows, cols = key
        return self.t[rows, cols.start - self.q_lo:cols.stop - self.q_lo]


build_program.kt_index = {}
QV_TAB = []


def ffn_phase(nc, P, es, sg, si, xin, yT, xt, hb, st_ss, junkb, n2bc, convp, identb, epsc,
              w_out, w_up, w_down, out_ap, out_row0, bankS, bankN, bankD, bankT, state,
              act, dve, pool, mm, tr, dma, rstd_chain, B):
    o_lo, o_hi = sg["o_lo"], sg["o_hi"]
    a0 = o_lo - 1
    wo, wd, x1, h2T, gT = B["wo"], B["wd"], B["x1"], B["h2T"], B["gT"]
    SCR = B["SCR"]
    if si == 0:
        k = 0
        for kc in range(8):
            s_ = k % 2; k += 1
            dma("sp", B["ot"][s_][:], w_out[kc * 128:(kc + 1) * 128, :], [], [f"ot{s_}"])
            dve(lambda e, s_=s_, kc=kc: e.tensor_copy(out=wo[:, kc, :], in_=B["ot"][s_][:]), [f"ot{s_}"], ["wo"])
        dma("pool", SCR["wo_s"], wo.rearrange("p c n -> p (c n)"), ["wo"], ["wo_s"])
        for fc in range(22):
            s_ = k % 2; k += 1
            dma("sp", B["ot"][s_][:], w_down[fc * 128:(fc + 1) * 128, :], [], [f"ot{s_}"])
            dve(lambda e, s_=s_, fc=fc: e.tensor_copy(out=wd[:, fc, :], in_=B["ot"][s_][:]), [f"ot{s_}"], ["wd"])
        dma("pool", SCR["wd_s"], wd.rearrange("p c n -> p (c n)"), ["wd"], ["wd_s"])
    else:
        pass

    yT_keys = [f"yT{c}" for c in range(8)]
    wur = w_up.rearrange("(c p) n -> p c n", p=128)
    PAIRS = [(bankS, "S"), (bankN, "N"), (bankD, "D")]

    blocks = []
    b0 = o_lo
    while b0 < o_hi:
        s_first = b0 - 1
        n = min(FBLK + 2, o_hi + 1 - s_first)
        tiles = []
        c = 0
        while c < n:
            m = min(128, n - c)
            tiles.append((c, m))
            c += m
        blocks.append(dict(b0=b0, s_first=s_first, n=n, tiles=tiles))
        b0 += FBLK

    def s4_tail(bi, ti, hs):
        (c, m) = blocks[bi]["tiles"][ti]
        tsl = state["tslot"]; state["tslot"] ^= 1
        tk = ["S3", "S2"][tsl]
        for kc in range(8):
            tr(bankT[tsl][:, kc * 128:kc * 128 + m], hb[hs][0:m, kc * 128:(kc + 1) * 128], identb[0:m, 0:m],
               [f"hb{hs}", "identb"], [tk])
        src = bankT[tsl].rearrange("p (c t) -> p c t", c=8)[:, :, 0:m]
        act(h2T[:, :, c:c + m], src, AF.Copy, [tk], ["h2T"])

    def s4_tile(bi, ti):
        blk = blocks[bi]
        (c, m) = blk["tiles"][ti]
        s_first, n, tiles = blk["s_first"], blk["n"], blk["tiles"]
        xs = state["xslot"]; state["xslot"] ^= 1
        xtile = xt[xs]
        dma("sp", xtile[0:m, :], xin[s_first + c:s_first + c + m, :], [], [f"xt{xs}"])
        pp = PAIRS[0]
        yc0 = s_first + c - a0
        for half in range(2):
            for kc in range(8):
                mm(pp[0][half][0:m, :], yT[:, kc, yc0:yc0 + m], wo[:, kc, half * 512:(half + 1) * 512], kc == 0, kc == 7,
                   yT_keys + ["wo"], [f"{pp[1]}{half}"])
        for half in range(2):
            dve(lambda e, pp=pp, half=half, m=m, ti=ti, xtile=xtile:
                e.tensor_tensor(out=x1[0:m, ti, half * 512:(half + 1) * 512], in0=pp[0][half][0:m, :],
                                in1=xtile[0:m, half * 512:(half + 1) * 512], op=ALU.add),
                [f"{pp[1]}{half}", f"xt{xs}"], [f"x1_{ti}"])
        c3 = 3 * (ti % 2)
        hs = ti % 2
        act(junkb[0:m, :], x1[0:m, ti, :], AF.Square, [f"x1_{ti}"], [f"ss{c3}"], accum=st_ss[0:m, c3:c3 + 1])
        rstd_chain(st_ss[0:m, c3:c3 + 1], st_ss[0:m, c3 + 2:c3 + 3], st_ss[0:m, c3 + 1:c3 + 2], 1.0 / D, f"ss{c3}", f"ss{c3 + 1}", f"ss{c3 + 2}")
        specials = []
        if ti == 0 and blk["b0"] == o_lo:
            specials.append((0, o_lo - 1))
        if ti == len(tiles) - 1 and s_first + n == o_hi + 1:
            specials.append((m - 1, o_hi))
        for (row, stok) in specials:
            col = len(QV_TAB)
            QV_TAB.append((si, row, stok))
            dve(lambda e, m=m, col=col, c3=c3: e.tensor_tensor(out=st_ss[0:m, c3 + 2:c3 + 3], in0=st_ss[0:m, c3 + 2:c3 + 3], in1=B["qv"][0:m, col:col + 1], op=ALU.mult),
                [f"ss{c3 + 2}", "qv"], [f"ss{c3 + 2}"])
        dve(lambda e, m=m, ti=ti, hs=hs, c3=c3: e.scalar_tensor_tensor(out=hb[hs][0:m, :], in0=x1[0:m, ti, :], scalar=st_ss[0:m, c3 + 2:c3 + 3], in1=n2bc[0:m, :],
                                                          op0=ALU.mult, op1=ALU.mult),
            [f"x1_{ti}", f"ss{c3 + 2}", "n2bc"], [f"hb{hs}"])
        return (bi, ti, hs)

    def u_phase(bi):
        n = blocks[bi]["n"]
        gkeys = [f"gT{f_}" for f_ in range(22)]
        pool(lambda e: e.memset(gT[:, :, 0:1], 0.0), [], gkeys)
        pool(lambda e, n=n: e.memset(gT[:, :, n - 1:n], 0.0), [], gkeys)
        for fc in range(22):
            ws = fc % 3
            if si == 0 and bi == 0:
                wf = B["ot"]
                for gi in range(2):
                    dma("sp", wf[gi].rearrange("p (c n) -> p c n", c=8), wur[:, :, gi * DFF + fc * 128:gi * DFF + (fc + 1) * 128],
                        [], [f"ot{gi}"])
                    (dve if gi == 0 else pool)(lambda e, ws=ws, gi=gi, wf=wf: e.tensor_copy(out=B["wu"][ws][:, :, gi * 128:(gi + 1) * 128],
                                                                   in_=wf[gi].rearrange("p (c n) -> p c n", c=8)),
                        [f"ot{gi}"], [f"wu{ws}"])
                dma("pool", SCR["wu_s"][fc], B["wu"][ws].rearrange("p c n -> p (c n)"), [f"wu{ws}"], [f"wu_s{fc}"])
            else:
                dma("sp", B["wu"][ws].rearrange("p c n -> p (c n)"), SCR["wu_s"][fc], [f"wu_s{fc}"], [f"wu{ws}"])
            pp = PAIRS[fc % 3]
            cs = fc % 3
            for gi in range(2):
                pb = pp[0][gi]
                pk = f"{pp[1]}{gi}"
                for kc in range(8):
                    mm(pb[:, 0:n], B["wu"][ws][:, kc, gi * 128:(gi + 1) * 128], h2T[:, kc, 0:n], kc == 0, kc == 7,
                       [f"wu{ws}", "h2T"], [pk])
                cbuf = (B["cg"] if gi == 0 else B["cv"])[cs]
                ckey = ("cg" if gi == 0 else "cv") + str(cs)
                fidx = fc + 22 * gi
                act(cbuf[:, 1:n - 1], pb[:, 1:n - 1], AF.Identity, [pk, "convp"], [ckey],
                    scale=convp[:, 44 + fidx:45 + fidx], bias=convp[:, 132 + fidx:133 + fidx])
                dve(lambda e, cbuf=cbuf, pb=pb, n=n, fidx=fidx: e.scalar_tensor_tensor(
                    out=cbuf[:, 1:n - 1], in0=pb[:, 0:n - 2], scalar=convp[:, fidx:fidx + 1], in1=cbuf[:, 1:n - 1],
                    op0=ALU.mult, op1=ALU.add), [pk, "convp", ckey], [ckey])
                dve(lambda e, cbuf=cbuf, pb=pb, n=n, fidx=fidx: e.scalar_tensor_tensor(
                    out=cbuf[:, 1:n - 1], in0=pb[:, 2:n], scalar=convp[:, 88 + fidx:89 + fidx], in1=cbuf[:, 1:n - 1],
                    op0=ALU.mult, op1=ALU.add), [pk, "convp", ckey], [ckey])
            act(B["sgt"][cs][:, 1:n - 1], B["cg"][cs][:, 1:n - 1], AF.Silu, [f"cg{cs}"], [f"sgt{cs}"])
            pool(lambda e, cs=cs, fc=fc, n=n: e.tensor_tensor(out=gT[:, fc, 1:n - 1], in0=B["sgt"][cs][:, 1:n - 1],
                                                            in1=B["cv"][cs][:, 1:n - 1], op=ALU.mult),
                 [f"sgt{cs}", f"cv{cs}"], [f"gT{fc}"])

    def d_tile(bi, ti):
        blk = blocks[bi]
        (c, m) = blk["tiles"][ti]
        s_first, tiles = blk["s_first"], blk["tiles"]
        pp = PAIRS[1 + ti % 2]
        for half in range(2):
            for fc in range(22):
                mm(pp[0][half][0:m, :], gT[:, fc, c:c + m], wd[:, fc, half * 512:(half + 1) * 512], fc == 0, fc == 21,
                   [f"gT{fc}", "wd"], [f"{pp[1]}{half}"])
        os_ = ti % 2
        for half in range(2):
            dve(lambda e, pp=pp, half=half, m=m, ti=ti, os_=os_:
                e.tensor_tensor(out=B["ot"][os_][0:m, half * 512:(half + 1) * 512], in0=pp[0][half][0:m, :],
                                in1=x1[0:m, ti, half * 512:(half + 1) * 512], op=ALU.add),
                [f"{pp[1]}{half}", f"x1_{ti}"], [f"ot{os_}"])
        r0 = 1 if ti == 0 else 0
        r1 = m - 1 if ti == len(tiles) - 1 else m
        if r1 > r0:
            tok0 = s_first + c + r0 - o_lo
            dma("pool", out_ap[out_row0 + tok0:out_row0 + tok0 + (r1 - r0), :], B["ot"][os_][r0:r1, :], [f"ot{os_}"], [])

    pending = None
    for ti in range(len(blocks[0]["tiles"])):
        nxt = s4_tile(0, ti)
        if pending is not None:
            s4_tail(*pending)
        pending = nxt
    s4_tail(*pending)
    for bi in range(len(blocks)):
        u_phase(bi)
        nt = len(blocks[bi]["tiles"])
        nt_next = len(blocks[bi + 1]["tiles"]) if bi + 1 < len(blocks) else 0
        pending = None
        for ti in range(max(nt, nt_next)):
            if ti < nt:
                d_tile(bi, ti)
            if ti < nt_next:
                nxt = s4_tile(bi + 1, ti)
                if pending is not None:
                    s4_tail(*pending)
                pending = nxt
        if pending is not None:
            s4_tail(*pending)


def build_all():
    nc, P, es, kt_tab = build_program()
    P.schedule()
    with ExitStack() as es2:
        sems = {e: es2.enter_context(nc.semaphore(f"sem_{e}")) for e in Prog.ENGS}
        dsems = {}
        for q in ("sp", "pool"):
            for i in range(P.NDS):
                dsems[(q, i)] = es2.enter_context(nc.semaphore(f"dsem_{q}_{i}"))
        block = es2.enter_context(nc.Block())
        run = emit_program(nc, P, sems, dsems)

        @block.sync
        def _(e):
            run("sp", e)

        @block.tensor
        def _(e):
            run("pe", e)

        @block.scalar
        def _(e):
            run("act", e)

        @block.vector
        def _(e):
            run("dve", e)

        @block.gpsimd
        def _(e):
            run("pool", e)
    es.close()
    return nc, kt_tab


_cache = {}


def kernel(x_prompt, x_sample, norm1, w_in, q_norm_a, k_norm_a, q_norm_b, k_norm_b, sink_b,
           out_norm_a, out_norm_b, w_out, norm2, w_up, conv_w, conv_b, w_down):
    if "nc" not in _cache:
        _cache["nc"] = build_all()
    nc, kt_tab = _cache["nc"]
    f = np.float32
    x_prompt = np.asarray(x_prompt, f)
    x_sample = np.asarray(x_sample, f)
    ident = np.eye(128, dtype=f)
    blk = np.zeros((128, 128), f)
    blk[:64, :64] = 1
    blk[64:, 64:] = 1
    i = np.arange(128)[:, None]
    c = np.arange(256)[None, :]
    dist = np.abs(i - c + 64)
    da = np.where(dist <= 64, -dist, -1e5).astype(f)
    c = np.arange(384)[None, :]
    dist = np.abs(i - c + 128)
    db = np.where(dist <= 128, -dist, -1e5).astype(f)
    pvec = np.zeros((128, 16), f)
    pvec[:, 0] = np.tile(np.asarray(q_norm_a, f)[0], 2)
    pvec[:, 1] = np.tile(np.asarray(k_norm_a, f)[0], 2)
    pvec[:, 2] = np.tile(np.asarray(q_norm_b, f)[0], 2)
    pvec[:, 3] = np.tile(np.asarray(k_norm_b, f)[0], 2)
    pvec[:, 4:8] = np.asarray(out_norm_a, f)[0].reshape(4, 128).T
    pvec[:, 8:12] = np.asarray(out_norm_b, f)[0].reshape(4, 128).T
    cw = np.asarray(conv_w, f)[0]
    cb = np.asarray(conv_b, f)[0]
    convp = np.concatenate([cw[0].reshape(44, 128).T, cw[1].reshape(44, 128).T, cw[2].reshape(44, 128).T,
                            cb.reshape(44, 128).T], axis=1).astype(f)
    common = dict(w_in=np.ascontiguousarray(np.asarray(w_in, f)[0]), w_out=np.ascontiguousarray(np.asarray(w_out, f)[0]),
                  w_up=np.ascontiguousarray(np.asarray(w_up, f)[0]), w_down=np.ascontiguousarray(np.asarray(w_down, f)[0]),
                  c_ident=ident, c_blk=blk, c_da=da, c_db=db, c_pvec=pvec, c_conv=np.ascontiguousarray(convp),
                  c_n1=np.asarray(norm1, f).reshape(1, D), c_n2=np.asarray(norm2, f).reshape(1, D),
                  c_sink=np.asarray(sink_b, f).reshape(1, 8))
    selm = np.zeros((128, 256), f)
    for j in range(64):
        selm[64 + j, j] = 1.0
        selm[j, 128 + 64 + j] = 1.0
    m2m = np.zeros((128, 8), f)
    for h in range(8):
        if h % 2 == 0:
            m2m[64:, h] = 1.0
        else:
            m2m[:64, h] = 1.0
    common.update(c_sel=selm, c_m2=m2m)
    in_maps = []
    for core in range(NCORES):
        xps, tlos = [], []
        for h in range(2):
            x0 = np.zeros((3328, D), f)
            t_lo = 2048 * core + 1024 * h - 1152
            lo, hi = max(0, t_lo), min(16384, t_lo + 3328)
            x0[lo - t_lo:hi - t_lo] = x_prompt[0, lo:hi]
            xps.append(x0)
            tlos.append(t_lo)
        xs = []
        for j in range(2):
            xx = np.zeros((2304, D), f)
            xx[128:2176] = x_sample[2 * core + j]
            xs.append(xx)
        vbt = np.zeros((128, NKT), f)
        assert len(kt_tab) <= NKT, len(kt_tab)
        for col, (si, pos) in enumerate(kt_tab):
            if si < 2:
                t = tlos[si] + np.asarray(pos)
                vbt[:len(pos), col] = np.where((t >= 0) & (t < 16384), 0.0, NEG)
        qvt = np.ones((128, 16), f)
        assert len(QV_TAB) <= 16
        for col, (si, row, stok) in enumerate(QV_TAB):
            if si < 2:
                t = tlos[si] + stok
                qvt[row, col] = 1.0 if (0 <= t < 16384) else 0.0
            else:
                qvt[row, col] = 0.0
        m = dict(common)
        m.update(x0=xps[0], x1=xps[1], x2=xs[0], x3=xs[1], c_vb=vbt, c_qv=qvt)
        in_maps.append(m)
    res = run_bass_kernel_spmd(nc, in_maps, core_ids=list(range(NCORES)))
    y_p = np.concatenate([r["out_p"] for r in res.results], axis=0).reshape(1, 16384, D)
    y_s = np.concatenate([r["out_s"].reshape(2, 2048, D) for r in res.results], axis=0)
    return (y_p.astype(f), y_s.astype(f))
```
